# Optimizing a Trainium2 kernel written in Bass

```python
import jax, jax.numpy as jnp
from jax import lax
import numpy as np

D_MODEL = 2048
BATCH = 2
SEQ = 16384
DEPTH = 2

CHUNK = 64
SB_HEAD_DIM = 128
SB_HEADS = 4
SB_WIDTH = SB_HEADS * SB_HEAD_DIM
Q_BLOCK = 128
CV_WIDTH = D_MODEL // 4
CV_FILTER = 31
SG_WIDTH = D_MODEL // 4
SG_GROUPS = 8
SG_CHUNK = 128
N_BRANCH = 3
FFN_HIDDEN = 2 * D_MODEL
FFN_FILTER = 3
IN_COLS = 3 * SB_WIDTH + 2 * CV_WIDTH + 2 * SG_WIDTH
EPS = 1e-6

kernel_name = "hybrid_stickbreak_conformer_gmlp_block"


def rms_norm(x, g):
    xf = x.astype(jnp.float32)
    y = xf * lax.rsqrt(jnp.mean(xf * xf, axis=-1, keepdims=True) + EPS)
    return (y * g.astype(jnp.float32)).astype(x.dtype)


def layer_norm(x, g, b):
    xf = x.astype(jnp.float32)
    mu = jnp.mean(xf, axis=-1, keepdims=True)
    var = jnp.mean(jnp.square(xf - mu), axis=-1, keepdims=True)
    y = (xf - mu) * lax.rsqrt(var + EPS)
    return (y * g.astype(jnp.float32) + b.astype(jnp.float32)).astype(x.dtype)


def causal_depthwise_conv(x, w, b):
    k = w.shape[0]
    out = lax.conv_general_dilated(
        x, w[:, None, :].astype(x.dtype), window_strides=(1,), padding=[(k - 1, 0)],
        dimension_numbers=("NWC", "WIO", "NWC"), feature_group_count=x.shape[-1])
    return out + b.astype(x.dtype)


def stick_breaking_block(q_blk, k_all, v_all, blk):
    bsz, h, nq, dh = q_blk.shape
    n_keys = k_all.shape[2]
    nkb = n_keys // nq
    z = jnp.einsum("bhqd,bhkd->bhqk", q_blk, k_all).astype(jnp.float32) * (1.0 / float(np.sqrt(dh)))
    q_pos = blk * nq + jnp.arange(nq, dtype=jnp.int32)
    key_pos = jnp.arange(n_keys, dtype=jnp.int32)
    mask = key_pos[None, :] < q_pos[:, None]
    ls_neg = jax.nn.log_sigmoid(-z)
    l1m = jnp.where(mask, ls_neg, 0.0).reshape(bsz, h, nq, nkb, nq)
    upper = jnp.triu(jnp.ones((nq, nq), jnp.float32), k=1)
    upper_b = jnp.triu(jnp.ones((nkb, nkb), jnp.float32), k=1)
    within = jnp.einsum("bhqnj,sj->bhqns", l1m, upper)
    later = jnp.einsum("bhqm,nm->bhqn", jnp.sum(l1m, axis=-1), upper_b)
    suffix = (within + later[..., None]).reshape(bsz, h, nq, n_keys)
    w = jnp.where(mask, jnp.exp(z + ls_neg + suffix), 0.0)
    return jnp.einsum("bhqk,bhkd->bhqd", w.astype(v_all.dtype), v_all)


def stick_breaking_attention(q, k, v):
    bsz, s_len, h, dh = q.shape
    nb = s_len // Q_BLOCK
    qh = q.transpose(0, 2, 1, 3)
    kh = k.transpose(0, 2, 1, 3)
    vh = v.transpose(0, 2, 1, 3)
    outs = []
    for blk in range(nb):
        lo, hi = blk * Q_BLOCK, (blk + 1) * Q_BLOCK
        outs.append(stick_breaking_block(qh[:, :, lo:hi], kh[:, :, :hi], vh[:, :, :hi], blk))
    out = jnp.concatenate(outs, axis=2)
    return out.transpose(0, 2, 1, 3).reshape(bsz, s_len, h * dh)


def conformer_conv(a, b, w_dw, b_dw, ln_g, ln_b):
    y = a * jax.nn.sigmoid(b)
    y = causal_depthwise_conv(y, w_dw, b_dw)
    return jax.nn.silu(layer_norm(y, ln_g, ln_b))


def spatial_gating(u, v, ln_g, ln_b, w_s, b_s):
    bsz, s_len, width = v.shape
    cg = width // SG_GROUPS
    pos = jnp.arange(SG_CHUNK, dtype=jnp.int32) // CHUNK
    mask = (pos[None, :] <= pos[:, None]).astype(w_s.dtype)
    vn = layer_norm(v, ln_g, ln_b).reshape(bsz, s_len // SG_CHUNK, SG_CHUNK, SG_GROUPS, cg)
    mixed = jnp.einsum("gts,bnsgc->bntgc", w_s * mask[None], vn)
    mixed = mixed + jnp.transpose(b_s)[:, :, None]
    return u * mixed.reshape(bsz, s_len, width)


def setup_inputs(seed: int = 0) -> dict:
    key = jax.random.key(seed)
    ks = jax.random.split(key, 32)
    f32 = jnp.float32
    L, D = DEPTH, D_MODEL

    def nrm(k, shape, scale):
        return jax.random.normal(k, shape, f32) * scale

    return {
        "x": nrm(ks[0], (BATCH, SEQ, D), 1.0),
        "c": nrm(ks[1], (BATCH, D), 1.0),
        "w_ada": nrm(ks[2], (L, D, 6 * D), 0.5 * D ** -0.5),
        "b_ada": nrm(ks[3], (L, 6 * D), 0.01),
        "g_norm1": 1.0 + nrm(ks[4], (L, D), 0.02),
        "g_norm2": 1.0 + nrm(ks[5], (L, D), 0.02),
        "w_in": nrm(ks[6], (L, D, IN_COLS), D ** -0.5),
        "sb_w_out": nrm(ks[7], (L, SB_WIDTH, D), SB_WIDTH ** -0.5),
        "cv_w_dw": nrm(ks[8], (L, CV_FILTER, CV_WIDTH), CV_FILTER ** -0.5),
        "cv_b_dw": nrm(ks[9], (L, CV_WIDTH), 0.01),
        "cv_ln_g": 1.0 + nrm(ks[10], (L, CV_WIDTH), 0.02),
        "cv_ln_b": nrm(ks[11], (L, CV_WIDTH), 0.01),
        "cv_w_out": nrm(ks[12], (L, CV_WIDTH, D), CV_WIDTH ** -0.5),
        "sg_ln_g": 1.0 + nrm(ks[13], (L, SG_WIDTH), 0.02),
        "sg_ln_b": nrm(ks[14], (L, SG_WIDTH), 0.01),
        "sg_w_s": nrm(ks[15], (L, SG_GROUPS, SG_CHUNK, SG_CHUNK), SG_CHUNK ** -0.5),
        "sg_b_s": 1.0 + nrm(ks[16], (L, SG_GROUPS, SG_CHUNK), 0.02),
        "sg_w_out": nrm(ks[17], (L, SG_WIDTH, D), SG_WIDTH ** -0.5),
        "w_gate": nrm(ks[18], (L, D, N_BRANCH * D), D ** -0.5),
        "b_gate": nrm(ks[19], (L, N_BRANCH * D), 0.01),
        "w_o": nrm(ks[20], (L, D, D), D ** -0.5),
        "ffn_w_up": nrm(ks[21], (L, D, 2 * FFN_HIDDEN), D ** -0.5),
        "ffn_w_dw": nrm(ks[22], (L, FFN_FILTER, FFN_HIDDEN), FFN_FILTER ** -0.5),
        "ffn_b_dw": nrm(ks[23], (L, FFN_HIDDEN), 0.01),
        "ffn_w_down": nrm(ks[24], (L, FFN_HIDDEN, D), FFN_HIDDEN ** -0.5),
        "g_final": 1.0 + nrm(ks[25], (D,), 0.02),
    }


def reference(x, c, w_ada, b_ada, g_norm1, g_norm2, w_in, sb_w_out, cv_w_dw, cv_b_dw,
              cv_ln_g, cv_ln_b, cv_w_out, sg_ln_g, sg_ln_b, sg_w_s, sg_b_s, sg_w_out,
              w_gate, b_gate, w_o, ffn_w_up, ffn_w_dw, ffn_b_dw, ffn_w_down, g_final):
    bsz, s_len, d = x.shape
    splits = [SB_WIDTH, 2 * SB_WIDTH, 3 * SB_WIDTH, 3 * SB_WIDTH + CV_WIDTH,
              3 * SB_WIDTH + 2 * CV_WIDTH, 3 * SB_WIDTH + 2 * CV_WIDTH + SG_WIDTH]
    cond = jax.nn.silu(c)
    for l in range(DEPTH):
        mod = (cond @ w_ada[l] + b_ada[l])[:, None, :]
        sh1, sc1, gt1, sh2, sc2, gt2 = jnp.split(mod, 6, axis=-1)

        h = rms_norm(x, g_norm1[l]) * (1.0 + sc1) + sh1
        proj = h @ w_in[l]
        q, k, v, cv_a, cv_b, sg_u, sg_v = jnp.split(proj, splits, axis=-1)

        hd = (bsz, s_len, SB_HEADS, SB_HEAD_DIM)
        y_a = stick_breaking_attention(q.reshape(hd), k.reshape(hd), v.reshape(hd)) @ sb_w_out[l]
        y_b = conformer_conv(cv_a, cv_b, cv_w_dw[l], cv_b_dw[l], cv_ln_g[l], cv_ln_b[l]) @ cv_w_out[l]
        y_c = spatial_gating(jax.nn.gelu(sg_u), jax.nn.gelu(sg_v), sg_ln_g[l], sg_ln_b[l],
                             sg_w_s[l], sg_b_s[l]) @ sg_w_out[l]

        gates = jax.nn.sigmoid(h @ w_gate[l] + b_gate[l]).reshape(bsz, s_len, N_BRANCH, d)
        merged = gates[:, :, 0] * y_a + gates[:, :, 1] * y_b + gates[:, :, 2] * y_c
        x = x + gt1 * (merged @ w_o[l])

        h2 = rms_norm(x, g_norm2[l]) * (1.0 + sc2) + sh2
        gate_pre, val = jnp.split(h2 @ ffn_w_up[l], 2, axis=-1)
        act = jax.nn.silu(causal_depthwise_conv(gate_pre, ffn_w_dw[l], ffn_b_dw[l])) * val
        x = x + gt2 * (act @ ffn_w_down[l])

    return rms_norm(x, g_final)
```

```python
import numpy as np
import ml_dtypes
import concourse.bass as bass
import concourse.mybir as mybir
from concourse.bass_utils import run_bass_kernel_spmd
from contextlib import ExitStack
from concourse.bass import IndirectOffsetOnAxis

F32 = mybir.dt.float32
BF16 = mybir.dt.bfloat16
AF = mybir.ActivationFunctionType
ALU = mybir.AluOpType

D = 2048
S = 16384
NB_ = 2
L_ = 2
NH = 4
DH = 128
EPS = 1e-6
NCORE = 8
TOKC = 4096
HALO = 128
ENGS = ("pe", "act", "dve", "pool", "sp")


class Buf:
    __slots__ = ("name", "lw", "rd", "x")

    def __init__(self, name="", x=False):
        self.name = name
        self.lw = None
        self.rd = {}
        self.x = x


class Prog:
    def __init__(self, nc, stack):
        self.nc = nc
        self.stack = stack
        self.ops = {e: [] for e in ENGS}
        self.cnt = {e: 0 for e in ENGS}
        self.ecnt = {e: 0 for e in ENGS}
        self.posof = {e: {} for e in ENGS}
        self.seen = {e: {} for e in ENGS}
        self.sems = {}
        self.dcnt = {}
        for e in ENGS:
            self.sems[e] = stack.enter_context(nc.semaphore("s_" + e))
        self.same_dist = 10 ** 9
        self.out_toks = []
        self.gstack = stack
        self.prefix = ""

    def begin_phase(self, prefix, stack):
        self.prefix = prefix
        self.stack = stack

    def barrier(self):
        for e in ENGS:
            waits = []
            for e2 in ENGS:
                if e2 != e and self.ecnt[e2] > 0:
                    self._need(e, (e2, self.ecnt[e2]), waits)
            for dname, v in self.dcnt.items():
                if v > 0:
                    self._need(e, (dname, v), waits)
            self.ops[e].append((None, waits, None))

    def end_phase(self):
        self.barrier()
        self.emit()
        self.ops = {e: [] for e in ENGS}

    def dsem(self, name):
        if name not in self.sems:
            self.sems[name] = self.gstack.enter_context(self.nc.semaphore("d_" + name))
            self.dcnt[name] = 0
        return name

    def sb(self, name, shape, dt):
        return self.stack.enter_context(self.nc.sbuf_tensor("sb_" + self.prefix + name, list(shape), dt))

    def ps(self, name, shape, dt=F32):
        return self.stack.enter_context(self.nc.psum_tensor("ps_" + self.prefix + name, list(shape), dt))

    def _need(self, e, tok, waits):
        if tok is None:
            return
        k, v = tok
        if k == e:
            if e == "pe":
                return
            if self.cnt[e] - self.posof[e][v] >= self.same_dist:
                return
        if self.seen[e].get(k, 0) >= v:
            return
        self.seen[e][k] = v
        waits.append((k, v))

    def op(self, e, fn, reads=(), writes=(), dma=None, dinc=16):
        if any(b.x for b in reads):
            writes = list(writes) + [b for b in reads if b.x and b not in writes]
            reads = [b for b in reads if not b.x]
        waits = []
        for b in reads:
            self._need(e, b.lw, waits)
        for b in writes:
            self._need(e, b.lw, waits)
            for k, v in b.rd.items():
                self._need(e, (k, v), waits)
        if dma is None:
            self.ecnt[e] += 1
            self.posof[e][self.ecnt[e]] = self.cnt[e]
            tok = (e, self.ecnt[e])
            inc = (e, 1)
        else:
            self.dcnt[dma] += dinc
            tok = (dma, self.dcnt[dma])
            inc = (dma, dinc)
        for b in reads:
            if b.rd.get(tok[0], 0) < tok[1]:
                b.rd[tok[0]] = tok[1]
        for b in writes:
            b.lw = tok
            b.rd = {}
        self.cnt[e] += 1
        self.ops[e].append((fn, waits, inc))
        return tok

    def finish(self, e="sp"):
        waits = []
        for t in self.out_toks:
            self._need(e, t, waits)
        self.ops[e].append((None, waits, None))

    def emit(self):
        nc = self.nc
        sems = self.sems
        allops = self.ops
        with nc.Block() as block:
            def run(eng, key):
                for fn, waits, inc in allops[key]:
                    for k, v in waits:
                        eng.wait_ge(sems[k], v)
                    if fn is not None:
                        fn(eng).then_inc(sems[inc[0]], inc[1])

            @block.tensor
            def _(eng):
                run(eng, "pe")

            @block.scalar
            def _(eng):
                run(eng, "act")

            @block.vector
            def _(eng):
                run(eng, "dve")

            @block.gpsimd
            def _(eng):
                run(eng, "pool")

            @block.sync
            def _(eng):
                run(eng, "sp")


class Rot:
    def __init__(self, P, name, n, shape, dt, psum=False):
        mk = P.ps if psum else P.sb
        self.t = [mk("%s%d" % (name, i), shape, dt) for i in range(n)]
        self.b = [Buf("%s%d" % (name, i), x=psum) for i in range(n)]
        self.i = 0
        self.P = P
        self.name = name
        self.cur = 0

    def next(self):
        r = (self.t[self.i], self.b[self.i])
        self.cur = self.i
        self.i = (self.i + 1) % len(self.t)
        return r

    @property
    def sem(self):
        return self.P.dsem("%s_%d" % (self.name, self.cur))


class K:
    def __init__(self, P):
        self.P = P

    def mm(self, out, lhsT, rhs, start, stop, reads, writes):
        self.P.op("pe", lambda e: e.matmul(out, lhsT=lhsT, rhs=rhs, start=start, stop=stop),
                  reads, writes)

    def tr(self, out, in_, ident, reads, writes):
        self.P.op("pe", lambda e: e.transpose(out=out, in_=in_, identity=ident), reads, writes)

    def act(self, out, in_, func, reads, writes, bias=None, scale=None, accum=None, eng="act"):
        kw = {}
        if bias is not None:
            kw["bias"] = bias
        if scale is not None:
            kw["scale"] = scale
        if accum is not None:
            kw["accum_out"] = accum
        self.P.op("act", lambda e: e.activation(out=out, in_=in_, func=func, **kw), reads, writes)

    def tt(self, out, in0, in1, op, reads, writes, eng="dve"):
        self.P.op(eng, lambda e: e.tensor_tensor(out=out, in0=in0, in1=in1, op=op), reads, writes)

    def ts(self, out, in0, s1, s2, op0, op1, reads, writes, eng="dve"):
        if s2 is None:
            self.P.op(eng, lambda e: e.tensor_scalar(out=out, in0=in0, scalar1=s1, scalar2=None, op0=op0),
                      reads, writes)
        else:
            self.P.op(eng, lambda e: e.tensor_scalar(out=out, in0=in0, scalar1=s1, scalar2=s2,
                                                     op0=op0, op1=op1), reads, writes)

    def stt(self, out, in0, scalar, in1, op0, op1, reads, writes, eng="dve"):
        self.P.op(eng, lambda e: e.scalar_tensor_tensor(out=out, in0=in0, scalar=scalar, in1=in1,
                                                        op0=op0, op1=op1), reads, writes)

    def cp(self, out, in_, reads, writes, eng="dve"):
        self.P.op(eng, lambda e: e.tensor_copy(out=out, in_=in_), reads, writes)

    def memset(self, ap, val, writes, eng="dve"):
        self.P.op(eng, lambda e: e.memset(ap, val), (), writes)

    def dma(self, q, out, in_, sem, reads, writes):
        return self.P.op(q, lambda e: e.dma_start(out=out, in_=in_), reads, writes, dma=sem)

    def asel(self, out, in_, n, cmp, base, cm, reads, writes, step=1):
        self.P.op("pool", lambda e: e.affine_select(out=out, in_=in_, pattern=[[step, n]], compare_op=cmp,
                                                    fill=0.0, base=base, channel_multiplier=cm),
                  reads, writes)


def emit_attn(nc, P, qT, kT, v, oT, seq=S, lag=2, gather=None, fused_out=None, nstream=2, side=None, side_every=24, pair=False):
    k_ = K(P)
    nblk = seq // 128
    nqt = seq // 512
    q_sb = P.sb("q_sb", [128, seq], BF16)
    k_sb = P.sb("k_sb", [128, seq], BF16)
    v_sb = P.sb("v_sb", [128, nblk, 128], BF16)
    CH = 2048 if gather is None else 4096
    nch = seq // CH
    qB = [Buf() for _ in range(nch)]
    kB = [Buf() for _ in range(nch)]
    vB = [Buf() for _ in range(nch)]
    if gather is None:
        for c in range(nch):
            k_.dma("sp", q_sb[:, c * CH:(c + 1) * CH], qT[:, c * CH:(c + 1) * CH], P.dsem("aq%d" % c), (), [qB[c]])
            k_.dma("sp", k_sb[:, c * CH:(c + 1) * CH], kT[:, c * CH:(c + 1) * CH], P.dsem("ak%d" % c), (), [kB[c]])
            nb_c = CH // 128
            k_.dma("sp", v_sb[:, c * nb_c:(c + 1) * nb_c, :],
                   v[c * CH:(c + 1) * CH, :].rearrange("(n p) d -> p n d", p=128), P.dsem("av%d" % c), (), [vB[c]])
    else:
        G2d, idx_sb, Bidx = gather
        for src in range(4):
            dsts = (q_sb[:, src * 4096:(src + 1) * 4096], k_sb[:, src * 4096:(src + 1) * 4096],
                    v_sb[:, src * 32:(src + 1) * 32, :].rearrange("p n d -> p (n d)"))
            for which in range(3):
                col = which * 4 + src
                P.op("pool", lambda e, o=dsts[which], col=col: e.indirect_dma_start(
                    out=o, out_offset=None, in_=G2d[:, :],
                    in_offset=IndirectOffsetOnAxis(ap=idx_sb[:, col:col + 1], axis=0)),
                    [Bidx], [(qB, kB, vB)[which][src]], dma=P.dsem("ag%d" % col))
    cf = P.sb("a_cf", [128, 128], F32)
    Bcf = Buf()
    triI = P.sb("a_tri", [128, 128], BF16)
    Btri = Buf()
    nones = P.sb("a_nones", [128, 128], BF16)
    Bno = Buf()
    k_.memset(cf[:], -1.0, [Bcf], eng="pool")
    k_.asel(triI[:], cf[:], 128, ALU.is_ge, 0, 1, [Bcf], [Btri], step=-1)
    k_.cp(nones[:], cf[:], [Bcf], [Bno], eng="pool")

    if fused_out is not None:
        zt = P.sb("a_zero", [128, 128], BF16)
        Bzt = Buf()
        k_.memset(zt[:], 0.0, [Bzt], eng="pool")
        k_.dma("sp", fused_out[1][0, :, :], zt[:], P.dsem("azero"), [Bzt], ())
    if pair:
        _emit_attn_pair(P, k_, locals(), seq, lag, fused_out, oT, side, side_every)
        return
    NSTR = nstream
    zr = Rot(P, "a_z", 8 - NSTR, [128, 512], F32, psum=True)
    accs = [Rot(P, "a_acc%d" % i, 1, [128, 512], F32, psum=True) for i in range(NSTR)]
    er = Rot(P, "a_e", 2 * NSTR, [128, 512], F32)
    lr = Rot(P, "a_l", NSTR * (lag + 2) + 1, [128, 512], BF16)
    wr = Rot(P, "a_w", 3 * NSTR, [128, 512], BF16)
    lcrs = [Rot(P, "a_lc%d" % i, 2, [128, 512], BF16) for i in range(NSTR)]
    osr = Rot(P, "a_os", 2, [128, 512], BF16)

    def stage1(qt, n):
        z, Bz = zr.next()
        e_, Be = er.next()
        l_, Bl = lr.next()
        qs = slice(qt * 512, (qt + 1) * 512)
        k_.mm(z[:], k_sb[:, n * 128:(n + 1) * 128], q_sb[:, qs], True, False,
              [kB[(n * 128) // CH], qB[(qt * 512) // CH]], [Bz])
        k_.act(e_[:], z[:], AF.Exp, [Bz], [Be])
        k_.act(l_[:], e_[:], AF.Ln, [Be], [Bl], bias=1.0)
        r = n - 4 * qt
        if r >= 0:
            k_.asel(l_[:], l_[:], 512, ALU.is_gt, -128 * r, -1, [Bl], [Bl])
        return (z, Bz, l_, Bl, r)

    def tile_gen(qt, sid):
        acc, Bacc = accs[sid].next()
        lcr = lcrs[sid]
        blocks = list(range(4 * qt + 3, -1, -1))
        st = {}
        lc_cur = None
        nsteps = len(blocks) + lag
        pend = None
        for i in range(nsteps):
            if i < len(blocks):
                st[i] = stage1(qt, blocks[i])
            j = i - lag
            if j >= 0:
                n = blocks[j]
                z, Bz, l_, Bl, r = st.pop(j)
                first = (j == 0)
                last = (n == 0)
                k_.mm(z[:], triI[:], l_[:], False, first, [Btri, Bl], [Bz])
                if not first:
                    k_.mm(z[:], nones[:], lc_cur[0][:], False, True, [Bno, lc_cur[1]], [Bz])
                if pend is not None:
                    k_.mm(*pend)
                    pend = None
                w_, Bw = wr.next()
                k_.act(w_[:], z[:], AF.Exp, [Bz], [Bw])
                if r >= 0:
                    k_.asel(w_[:], w_[:], 512, ALU.is_gt, -128 * r, -1, [Bw], [Bw])
                pend = (acc[:], v_sb[:, n, :], w_[:], first, last, [vB[(n * 128) // CH], Bw], [Bacc])
                if not last:
                    lc_n = lcr.next()
                    if first:
                        k_.cp(lc_n[0][:], l_[:], [Bl], [lc_n[1]])
                    else:
                        k_.tt(lc_n[0][:], lc_cur[0][:], l_[:], ALU.add, [lc_cur[1], Bl], [lc_n[1]])
                    lc_cur = lc_n
            yield
        k_.mm(*pend)
        o_, Bo = osr.next()
        k_.cp(o_[:], acc[:], [Bacc], [Bo])
        if fused_out is None:
            P.out_toks.append(k_.dma("sp", oT[:, qt * 512:(qt + 1) * 512], o_[:], osr.sem, [Bo], ()))
        else:
            SA, SH = fused_out
            dest, tl = qt // 8, qt % 8
            k_.dma("sp", SA[dest, :, tl * 512:(tl + 1) * 512], o_[:], osr.sem, [Bo], ())
            if tl == 7 and dest < 3:
                k_.dma("sp", SH[dest + 1, :, :], o_[:, 384:512], osr.sem, [Bo], ())

    pending = list(range(nqt - 1, -1, -1))
    active = {}
    nstep = 0
    while pending or active:
        nstep += 1
        if side is not None and nstep % side_every == 0:
            next(side, None)
        for sid in range(NSTR):
            if sid not in active and pending:
                active[sid] = tile_gen(pending.pop(0), sid)
        for sid in list(active.keys()):
            try:
                next(active[sid])
            except StopIteration:
                del active[sid]
    if side is not None:
        for _ in side:
            pass


def _emit_attn_pair(P, k_, env, seq, lag, fused_out, oT, side, side_every):
    q_sb, k_sb, v_sb = env["q_sb"], env["k_sb"], env["v_sb"]
    qB, kB, vB, CH = env["qB"], env["kB"], env["vB"], env["CH"]
    triI, Btri, nones, Bno = env["triI"], env["Btri"], env["nones"], env["Bno"]
    nqt = seq // 512
    zr = Rot(P, "a_z2", 3, [128, 1024], F32, psum=True)
    NSTR = env["nstream"]
    accs = [Rot(P, "a_acc%d" % i, 2 // NSTR, [128, 512], F32, psum=True) for i in range(NSTR)]
    er = Rot(P, "a_e2", 2 * NSTR, [128, 1024], F32)
    lr = Rot(P, "a_l2", NSTR * (lag + 1) + 1, [128, 1024], BF16)
    wr = Rot(P, "a_w2", 2 * NSTR + 1, [128, 1024], BF16)
    lcrs = [Rot(P, "a_lc%d" % i, 4, [128, 512], BF16) for i in range(NSTR)]
    osr = Rot(P, "a_os", 2, [128, 512], BF16)
    H = (slice(0, 512), slice(512, 1024))

    def tile_gen(qt, sid):
        acc, Bacc = accs[sid].next()
        lcr = lcrs[sid]
        blocks = list(range(4 * qt + 3, -1, -1))
        pairs = [(blocks[2 * i], blocks[2 * i + 1]) for i in range(len(blocks) // 2)]
        qs = slice(qt * 512, (qt + 1) * 512)
        st = {}
        lc_cur = None
        pend = []
        for i in range(len(pairs) + lag):
            if i < len(pairs):
                z, Bz = zr.next()
                e_, Be = er.next()
                l_, Bl = lr.next()
                for hh, n in enumerate(pairs[i]):
                    k_.mm(z[:, H[hh]], k_sb[:, n * 128:(n + 1) * 128], q_sb[:, qs], True, False,
                          [kB[(n * 128) // CH], qB[(qt * 512) // CH]], [Bz])
                k_.act(e_[:], z[:], AF.Exp, [Bz], [Be])
                k_.act(l_[:], e_[:], AF.Ln, [Be], [Bl], bias=1.0)
                for hh, n in enumerate(pairs[i]):
                    r = n - 4 * qt
                    if r >= 0:
                        k_.asel(l_[:, H[hh]], l_[:, H[hh]], 512, ALU.is_gt, -128 * r, -1, [Bl], [Bl])
                st[i] = (z, Bz, l_, Bl, pairs[i])
            j = i - lag
            if j >= 0:
                z, Bz, l_, Bl, (n0, n1) = st.pop(j)
                first = (j == 0)
                last = (n1 == 0)
                k_.mm(z[:, H[0]], triI[:], l_[:, H[0]], False, first, [Btri, Bl], [Bz])
                if not first:
                    k_.mm(z[:, H[0]], nones[:], lc_cur[0][:], False, True, [Bno, lc_cur[1]], [Bz])
                lc_mid = lcr.next()
                if first:
                    k_.cp(lc_mid[0][:], l_[:, H[0]], [Bl], [lc_mid[1]])
                else:
                    k_.tt(lc_mid[0][:], lc_cur[0][:], l_[:, H[0]], ALU.add, [lc_cur[1], Bl], [lc_mid[1]])
                for p in pend:
                    k_.mm(*p)
                pend = []
                k_.mm(z[:, H[1]], triI[:], l_[:, H[1]], False, False, [Btri, Bl], [Bz])
                k_.mm(z[:, H[1]], nones[:], lc_mid[0][:], False, True, [Bno, lc_mid[1]], [Bz])
                if not last:
                    lc_new = lcr.next()
                    k_.tt(lc_new[0][:], lc_mid[0][:], l_[:, H[1]], ALU.add, [lc_mid[1], Bl], [lc_new[1]])
                    lc_cur = lc_new
                w_, Bw = wr.next()
                k_.act(w_[:], z[:], AF.Exp, [Bz], [Bw])
                for hh, n in enumerate((n0, n1)):
                    r = n - 4 * qt
                    if r >= 0:
                        k_.asel(w_[:, H[hh]], w_[:, H[hh]], 512, ALU.is_gt, -128 * r, -1, [Bw], [Bw])
                pend = [(acc[:], v_sb[:, n0, :], w_[:, H[0]], first, False, [vB[(n0 * 128) // CH], Bw], [Bacc]),
                        (acc[:], v_sb[:, n1, :], w_[:, H[1]], False, last, [vB[(n1 * 128) // CH], Bw], [Bacc])]
            yield
        for p in pend:
            k_.mm(*p)
        o_, Bo = osr.next()
        k_.cp(o_[:], acc[:], [Bacc], [Bo])
        if fused_out is None:
            P.out_toks.append(k_.dma("sp", oT[:, qt * 512:(qt + 1) * 512], o_[:], osr.sem, [Bo], ()))
        else:
            SA, SH = fused_out
            dest, tl = qt // 8, qt % 8
            k_.dma("sp", SA[dest, :, tl * 512:(tl + 1) * 512], o_[:], osr.sem, [Bo], ())
            if tl == 7 and dest < 3:
                k_.dma("sp", SH[dest + 1, :, :], o_[:, 384:512], osr.sem, [Bo], ())

    pending = list(range(nqt - 1, -1, -1))
    active = {}
    nstep = 0
    while pending or active:
        nstep += 1
        if side is not None and nstep % side_every == 0:
            next(side, None)
        for sid in range(NSTR):
            if sid not in active and pending:
                active[sid] = tile_gen(pending.pop(0), sid)
        for sid in list(active.keys()):
            try:
                next(active[sid])
            except StopIteration:
                del active[sid]
    if side is not None:
        for _ in side:
            pass


def build_attn(seq=S):
    nc = bass.Bass("TRN2", target_bir_lowering=False)
    qT = nc.dram_tensor("qT", [128, seq], BF16, kind="ExternalInput").ap()
    kT = nc.dram_tensor("kT", [128, seq], BF16, kind="ExternalInput").ap()
    v = nc.dram_tensor("v", [seq, 128], BF16, kind="ExternalInput").ap()
    oT = nc.dram_tensor("oT", [128, seq], BF16, kind="ExternalOutput").ap()
    with ExitStack() as st:
        P = Prog(nc, st)
        emit_attn(nc, P, qT, kT, v, oT, seq=seq)
        P.finish()
        P.emit()
    return nc


class WRing:
    def __init__(self, P, name, ns, shape, units):
        self.P = P
        self.ns = ns
        self.t = [P.sb("%s%d" % (name, i), shape, BF16) for i in range(ns)]
        self.b = [Buf() for _ in range(ns)]
        self.sem = [P.dsem("%s%d" % (name, i)) for i in range(ns)]
        self.units = units
        self.nl = 0
        self.na = 0
        self.rel = set()
        self.pump()

    def pump(self):
        while self.nl < len(self.units) and (self.nl < self.ns or (self.nl - self.ns) in self.rel):
            s = self.nl % self.ns
            first = True
            tok = None
            for o, i in self.units[self.nl](self.t[s]):
                tok = self.P.op("pool", lambda e, o=o, i=i: e.dma_start(out=o, in_=i), (),
                                [self.b[s]] if first else (), dma=self.sem[s])
                first = False
            self.b[s].lw = tok
            self.nl += 1

    def acquire(self):
        k = self.na
        self.na += 1
        assert k < self.nl, "weight ring deadlock"
        return k, self.t[k % self.ns], self.b[k % self.ns]

    def release(self, k):
        self.rel.add(k)
        self.pump()


def wunit(w, r0, nk, c0, ncol=512):
    def f(t):
        src = w[r0:r0 + nk * 128, c0:c0 + ncol].rearrange("(kc p) n -> p kc n", p=128)
        return [(t[:, k0:k0 + 4, 0:ncol], src[:, k0:k0 + 4, :]) for k0 in range(0, nk, 4)]
    return f


def tile_units(d, pre, qkv):
    big, small = [], []
    if pre:
        wi, wg, wo, wu, wd = d["w_in"], d["w_gate"], d["w_o"], d["w_up"], d["w_dn"]
        for c0 in (2048, 1536, 2560, 3072):
            big.append(wunit(wi, 0, 16, c0))
        for cg in range(4):
            for i in range(3):
                big.append(wunit(wg, 0, 16, i * 2048 + cg * 512))
                small.append(wunit(d[("sbo", "cvo", "sgo")[i]], 0, 4, cg * 512))
        for cg in range(4):
            big.append(wunit(wo, 0, 16, cg * 512))
        for kh in range(2):
            for u in range(4):
                big.append(wunit(wu, 0, 16, (kh * 4 + u) * 512))
                big.append(wunit(wu, 0, 16, 4096 + (kh * 4 + u) * 512))
            for cg in range(4):
                big.append(wunit(wd, kh * 2048, 16, cg * 512))
    if qkv:
        for c0 in (0, 512, 1024):
            big.append(wunit(d["w_in_n"], 0, 16, c0))
    return big, small


def convert_gen(P, big, small, Wb, Wsb):
    k_ = K(P)
    rb_ = Rot(P, "cvb", 2, [128, 16, 512], BF16)
    rs_ = Rot(P, "cvs", 2, [128, 4, 512], BF16)
    for fns, rot, dst in ((big, rb_, Wb), (small, rs_, Wsb)):
        for u, fn in enumerate(fns):
            t, Bt = rot.next()
            first = True
            tok = None
            for o, i in fn(t):
                tok = P.op("pool", lambda e, o=o, i=i: e.dma_start(out=o, in_=i), (), [Bt] if first else (),
                           dma=rot.sem)
                first = False
            Bt.lw = tok
            k_.dma("sp", dst[u], t[:], P.dsem(rot.name + "_w%d" % rot.cur), [Bt], ())
            yield


PP_G1, PP_G2, PP_BG, PP_CVW, PP_CVB, PP_CVG, PP_CVBT, PP_FFW, PP_FFB = 0, 16, 32, 80, 204, 208, 212, 216, 312
NPP = 344


def emit_P(nc, P, d, pre, qkv, fin, ntile=8, halo=True, stop=None):
    k_ = K(P)
    xin = d["xin"]
    tiles = []
    off = 0
    if halo:
        tiles.append((0, 1, True))
        off = 128
    for i in range(ntile):
        tiles.append((off + i * 512, 4, False))

    dc = P.dsem("c")
    ident = P.sb("ident", [128, 128], BF16)
    Bid = Buf()
    onesf = P.sb("onesf", [128, 128], F32)
    Bof = Buf()
    onesb = P.sb("onesb", [128, 128], BF16)
    Bob = Buf()
    k_.memset(onesf[:], 1.0, [Bof], eng="dve")
    k_.cp(onesb[:], onesf[:], [Bof], [Bob])
    P.op("pool", lambda e: e.affine_select(out=ident[:], in_=onesf[:], pattern=[[1, 128]],
                                           compare_op=ALU.is_equal, fill=0.0, base=0, channel_multiplier=-1),
         [Bof], [Bid])
    Bpar = Buf()
    hm = P.sb("hmask", [128, 1], F32)
    k_.dma("sp", hm[:], d["hmask"][:, :], dc, (), [Bpar])

    def norm_params(pp_t, modt, gcol, shcol, sccol, name):
        a = P.sb(name + "_a", [128, 16], F32)
        k_.ts(a[:], modt[:, sccol:sccol + 16], 1.0, None, ALU.add, None, [Bpar], [Bpar])
        k_.tt(a[:], a[:], pp_t[:, gcol:gcol + 16], ALU.mult, [Bpar], [Bpar])
        return a

    if pre:
        pp = P.sb("pp", [128, NPP], F32)
        k_.dma("sp", pp[:], d["pp"][:, :], dc, (), [Bpar])
        modt = P.sb("modt", [128, 96], F32)
        k_.dma("sp", modt[:], d["modpp"][:, :], dc, (), [Bpar])
        rb = P.sb("rb", [128, 3, 512], F32)
        k_.dma("sp", rb[:], d["rb"][:, :, :], dc, (), [Bpar])
        wsT = P.sb("wsT", [128, 8, 128], BF16)
        k_.dma("pool", wsT[:], d["wsT"][:, :, :], P.dsem("cws"), (), [Bpar])
        k_.memset(wsT[64:128, :, 0:64], 0.0, [Bpar], eng="dve")
        a1 = norm_params(pp, modt, PP_G1, 0, 16, "n1")
        a2 = norm_params(pp, modt, PP_G2, 48, 64, "n2")
        b1 = modt[:, 0:16]
        b2 = modt[:, 48:64]
    if qkv:
        ppn = P.sb("ppn", [128, 16], F32)
        k_.dma("sp", ppn[:], d["g1n"][:, :], dc, (), [Bpar])
        modn = P.sb("modn", [128, 32], F32)
        k_.dma("sp", modn[:], d["modn"][:, :], dc, (), [Bpar])
        a1n = norm_params(ppn, modn, 0, 0, 16, "nn")
        b1n = modn[:, 0:16]

    if stop == 'params':
        return
    big_units = []
    small_units = []
    for (_, _, ish) in tiles:
        if "Wb" in d:
            nb_pre = 44 if pre else 0
            for u in range(nb_pre):
                if ish and u >= 20 and ((u - 20) % 12 >= 8 or (u - 20) % 2 == 1):
                    continue
                big_units.append(lambda t, u=u: [(t[:, 0:8, :], d["Wb"][u, :, 0:8, :]), (t[:, 8:16, :], d["Wb"][u, :, 8:16, :])])
            if pre:
                for u in range(12):
                    small_units.append(lambda t, u=u: [(t[:, :, :], d["Wsb"][u, :, :, :])])
            if qkv and not ish:
                for u in range(3):
                    big_units.append(lambda t, u=u: [(t[:, 0:8, :], d["Wb"][nb_pre + u, :, 0:8, :]),
                                                     (t[:, 8:16, :], d["Wb"][nb_pre + u, :, 8:16, :])])
        else:
            bu, su = tile_units(d, pre, qkv and not ish)
            if ish:
                bu = [f for u, f in enumerate(bu) if not (u >= 20 and ((u - 20) % 12 >= 8 or (u - 20) % 2 == 1))]
            big_units += bu
            small_units += su
    resident = (not pre) and qkv and "Wb" not in d
    if resident:
        big_units = big_units[0:3]

        class _Res:
            def __init__(self, ring):
                self.ring = ring
                self.k = 0

            def acquire(self):
                i = self.k % 3
                self.k += 1
                return i, self.ring.t[i], self.ring.b[i]

            def release(self, k):
                pass
    WB = WRing(P, "wb", 3, [128, 16, 512], big_units)
    if resident:
        WB = _Res(WB)
    if pre:
        WS = WRing(P, "ws", 2, [128, 4, 512], small_units)

    x_sb = P.sb("x_sb", [128, 4, 2048], F32)
    Bx = [Buf() for _ in range(4)]
    xsr = Rot(P, "xs", 1, [128, 2048], BF16)
    h_sb = P.sb("h_sb", [128, 16, 512], BF16)
    Bh = [Buf() for _ in range(16)]
    npst = 2
    pstr = Rot(P, "pst", npst, [128, 512], BF16, psum=True)
    bank = Rot(P, "bk", 8 - npst, [128, 512], F32, psum=True)
    tmpr = Rot(P, "tmp", 3, [128, 512], F32)
    smallr = Rot(P, "sml", 4, [128, 16], F32)
    dx = P.dsem("x")
    big = P.sb("big", [128, 16, 512], BF16)
    Bbig = [Buf() for _ in range(16)]
    xs4 = big[:].rearrange("p a b -> p (a b)")
    if pre:
        f32b = P.sb("f32b", [128, 4, 512], F32)
        Bf32 = [Buf() for _ in range(4)]
        glu = P.sb("glu", [128, 4, 30 + 512], BF16)
        cdgr = Rot(P, "cdiag", 2, [128, 31, 128], BF16)
        Bglu = [Buf() for _ in range(4)]
        k_.memset(glu[:], 0.0, Bglu, eng="dve")
        fh = P.sb("ffn_halo", [128, 32, 2], F32)
        Bfh = [Buf() for _ in range(32)]
        k_.memset(fh[:], 0.0, Bfh, eng="dve")
        ugr = Rot(P, "ug", 2, [128, 512], F32)
        vgr = Rot(P, "vg", 2, [128, 512], F32)
        conf = P.sb("conf", [128, 4, 512], BF16)
        Bconf = [Buf() for _ in range(4)]
        sgT = P.sb("sgT", [128, 4, 512], BF16)
        BsgT = Buf()
        at_sb = P.sb("at_sb", [128, 4, 512], BF16)
        Bat = Buf()
        b16r = Rot(P, "b16", 4, [128, 512], BF16)
        gpr = Rot(P, "gp", 2, [128, 2 + 512], F32)
        gtr = Rot(P, "gt", 2, [128, 512], F32)
        str_ = Rot(P, "st6", 2, [128, 6], F32)
    if qkv:
        stg = Rot(P, "stg", 3, [128, 512], BF16)

    def rstd_of(ss, n, inv_n, reads):
        r, Br = smallr.next()
        k_.ts(r[:, 0:n], ss, inv_n, EPS, ALU.mult, ALU.add, reads, [Br])
        k_.act(r[:, 0:n], r[:, 0:n], AF.Ln, [Br], [Br])
        k_.act(r[:, 0:n], r[:, 0:n], AF.Exp, [Br], [Br], scale=-0.5)
        return r, Br

    def norm_to_h(nb, a, b):
        T = nb * 128
        ss, Bss = smallr.next()
        for j in range(nb):
            k_.act(xs4[:, j * 2048:(j + 1) * 2048], x_sb[:, j, :], AF.Square, [Bx[j]],
                   Bbig[4 * j:4 * j + 4] + [Bss], accum=ss[:, j:j + 1])
        r, Br = rstd_of(ss[:, 0:nb], nb, 1.0 / D, [Bss])
        for j in range(nb):
            if j % 2 == 0:
                k_.act(xs4[:, j * 2048:(j + 1) * 2048], x_sb[:, j, :], AF.Identity, [Bx[j], Br],
                       Bbig[4 * j:4 * j + 4], scale=r[:, j:j + 1])
            else:
                k_.ts(xs4[:, j * 2048:(j + 1) * 2048], x_sb[:, j, :], r[:, j:j + 1], None, ALU.mult, None,
                      [Bx[j], Br], Bbig[4 * j:4 * j + 4])
        for c in range(16):
            pt, Bpt = pstr.next()
            for j in range(nb):
                k_.tr(pt[:, j * 128:(j + 1) * 128], xs4[:, j * 2048 + c * 128:j * 2048 + (c + 1) * 128], ident[:],
                      [Bbig[4 * j + c // 4], Bid], [Bpt])
            if c % 2 == 0:
                k_.act(h_sb[:, c, 0:T], pt[:, 0:T], AF.Identity, [Bpt, Bpar], [Bh[c]],
                       bias=b[:, c:c + 1], scale=a[:, c:c + 1])
            else:
                k_.ts(h_sb[:, c, 0:T], pt[:, 0:T], a[:, c:c + 1], b[:, c:c + 1], ALU.mult, ALU.add,
                      [Bpt, Bpar], [Bh[c]])

    def fm_group(wt, Bw, oc, nk, rhs_fn, rhs_bufs, T):
        bk, Bb = bank.next()
        for kc in range(nk):
            k_.mm(bk[:, 0:T], wt[:, kc, oc * 128:(oc + 1) * 128], rhs_fn(kc), kc == 0, kc == nk - 1,
                  [Bw] + rhs_bufs(kc), [Bb])
        return bk, Bb

    def tm_group(wt, Bw, j, nk, lhs_fn, lhs_bufs, bkB=None, first=True, lastg=True, kbase=0):
        if bkB is None:
            bkB = bank.next()
        bk, Bb = bkB
        for kc in range(nk):
            k_.mm(bk[:], lhs_fn(kbase + kc)[:, j * 128:(j + 1) * 128], wt[:, kc, :],
                  first and kc == 0, lastg and kc == nk - 1, [Bw] + lhs_bufs(kbase + kc), [Bb])
        return bk, Bb

    def resid_add(bk, Bb, j, cg, gtt, Bgt):
        t, Bt = tmpr.next()
        k_.tt(t[:], bk[:], gtt[:], ALU.mult, [Bb, Bgt], [Bt])
        xs_ = x_sb[:, j, cg * 512:(cg + 1) * 512]
        k_.tt(xs_, xs_, t[:], ALU.add, [Bt, Bx[j]], [Bx[j]])

    def gelu_tanh(dst, Bdst, bk, Bb, eng="dve"):
        k_.act(dst[:], bk[:], AF.Copy, [Bb], [Bdst])
        t, Bt = tmpr.next()
        k_.tt(t[:], dst[:], dst[:], ALU.mult, [Bdst], [Bt], eng=eng)
        k_.ts(t[:], t[:], 0.044715, 1.0, ALU.mult, ALU.add, [Bt], [Bt], eng=eng)
        k_.tt(t[:], t[:], dst[:], ALU.mult, [Bt, Bdst], [Bt], eng=eng)
        k_.act(t[:], t[:], AF.Sigmoid, [Bt], [Bt], scale=1.5957691216057308)
        k_.tt(dst[:], dst[:], t[:], ALU.mult, [Bt, Bdst], [Bdst], eng=eng)

    hfn = lambda kc: h_sb[:, kc, :]
    hbuf = lambda kc: [Bh[kc]]

    for (tok0, nb, ish) in tiles:
        T = nb * 128
        for j in range(nb):
            k_.dma("sp", x_sb[:, j, :], xin[tok0 + j * 128:tok0 + (j + 1) * 128, :], P.dsem("x%d" % j), (), [Bx[j]])
        if pre:
            k_.dma("sp", at_sb[:, :, 0:T], d["attnT"][:, :, tok0:tok0 + T].rearrange("h p t -> p h t"),
                   P.dsem("at"), (), [Bat])
            if stop == 'xload':
                return
            norm_to_h(nb, a1, b1)
            if stop == 'norm1':
                return
            def build_diag(i):
                cdg, Bcdg = cdgr.next()
                for j in range(31):
                    k_.ts(cdg[:, j, :], ident[:], pp[:, PP_CVW + i * 31 + j:PP_CVW + i * 31 + j + 1], None,
                          ALU.mult, None, [Bid, Bpar], [Bcdg])
                return cdg, Bcdg
            diags = [build_diag(0), build_diag(1)]
            ku, wt, Bw = WB.acquire()
            for i in range(4):
                bk, Bb = fm_group(wt, Bw, i, 16, lambda kc: h_sb[:, kc, 0:T], hbuf, T)
                k_.act(glu[:, i, 30:30 + T], bk[:, 0:T], AF.Sigmoid, [Bb], [Bglu[i]])
            WB.release(ku)
            ku, wt, Bw = WB.acquire()
            for i in range(4):
                bk, Bb = fm_group(wt, Bw, i, 16, lambda kc: h_sb[:, kc, 0:T], hbuf, T)
                k_.tt(glu[:, i, 30:30 + T], glu[:, i, 30:30 + T], bk[:, 0:T], ALU.mult, [Bb, Bglu[i]], [Bglu[i]])
            WB.release(ku)
            if stop == 'glu':
                return
            s1, Bs1 = bank.next()
            s2, Bs2 = bank.next()
            pend_stats = []
            for i in range(4):
                y = f32b[:, i, 0:T]
                cdg, Bcdg = diags[i]
                cb, Bcb = bank.next()
                for j in range(31):
                    k_.mm(cb[:, 0:T], cdg[:, j, :], glu[:, i, j:j + T], j == 0, j == 30, [Bcdg, Bglu[i]], [Bcb])
                for pa in pend_stats:
                    k_.mm(*pa)
                pend_stats = []
                if i + 2 < 4:
                    diags.append(build_diag(i + 2))
                k_.act(y, cb[:, 0:T], AF.Identity, [Bcb, Bpar], [Bf32[i]], bias=pp[:, PP_CVB + i:PP_CVB + i + 1])
                if ish:
                    k_.ts(glu[:, i, 0:30], glu[:, i, T:T + 30], hm[:, 0:1], None, ALU.mult, None,
                          [Bglu[i], Bpar], [Bglu[i]])
                else:
                    k_.cp(glu[:, i, 0:30], glu[:, i, T:T + 30], [Bglu[i]], [Bglu[i]])
                yb, Byb = b16r.next()
                k_.cp(yb[:, 0:T], y, [Bf32[i]], [Byb])
                ysq, Bysq = b16r.next()
                k_.act(ysq[:, 0:T], y, AF.Square, [Bf32[i]], [Bysq])
                pend_stats = [(s1[:, 0:T], onesb[:], yb[:, 0:T], i == 0, i == 3, [Bob, Byb], [Bs1]),
                              (s2[:, 0:T], onesb[:], ysq[:, 0:T], i == 0, i == 3, [Bob, Bysq], [Bs2])]
            for pa in pend_stats:
                k_.mm(*pa)
            mean, Bmean = tmpr.next()
            k_.ts(mean[:, 0:T], s1[:, 0:T], 1.0 / 512, None, ALU.mult, None, [Bs1], [Bmean])
            var, Bvar = tmpr.next()
            k_.tt(var[:, 0:T], mean[:, 0:T], mean[:, 0:T], ALU.mult, [Bmean], [Bvar])
            k_.stt(var[:, 0:T], s2[:, 0:T], 1.0 / 512, var[:, 0:T], ALU.mult, ALU.subtract, [Bs2, Bvar], [Bvar])
            k_.ts(var[:, 0:T], var[:, 0:T], EPS, None, ALU.add, None, [Bvar], [Bvar])
            k_.act(var[:, 0:T], var[:, 0:T], AF.Ln, [Bvar], [Bvar])
            k_.act(var[:, 0:T], var[:, 0:T], AF.Exp, [Bvar], [Bvar], scale=-0.5)
            for i in range(4):
                y = f32b[:, i, 0:T]
                k_.tt(y, y, mean[:, 0:T], ALU.subtract, [Bf32[i], Bmean], [Bf32[i]])
                k_.tt(y, y, var[:, 0:T], ALU.mult, [Bf32[i], Bvar], [Bf32[i]])
                k_.act(conf[:, i, 0:T], y, AF.Silu, [Bf32[i], Bpar], [Bconf[i]],
                       bias=pp[:, PP_CVBT + i:PP_CVBT + i + 1], scale=pp[:, PP_CVG + i:PP_CVG + i + 1])
            if stop == 'conf':
                return
            kuu, wtu, Bwu = WB.acquire()
            kuv, wtv, Bwv = WB.acquire()
            def sg_a(j):
                bu, Bbu = tm_group(wtu, Bwu, j, 16, hfn, hbuf)
                ug, Bug = ugr.next()
                gelu_tanh(ug, Bug, bu, Bbu)
                bv, Bbv = tm_group(wtv, Bwv, j, 16, hfn, hbuf)
                vg, Bvg = vgr.next()
                gelu_tanh(vg, Bvg, bv, Bbv, eng="pool")
                return ug, Bug, vg, Bvg

            def sg_b(j, ug, Bug, vg, Bvg):
                st6, Bst = str_.next()
                P.op("dve", lambda e, st6=st6, vg=vg: e.bn_stats(out=st6[:], in_=vg[:]), [Bvg], [Bst])
                mv, Bmv = smallr.next()
                P.op("dve", lambda e, st6=st6, mv=mv: e.bn_aggr(out=mv[:, 0:2], in_=st6[:]), [Bst], [Bmv])
                k_.ts(mv[:, 2:3], mv[:, 1:2], EPS, None, ALU.add, None, [Bmv], [Bmv])
                k_.act(mv[:, 2:3], mv[:, 2:3], AF.Ln, [Bmv], [Bmv])
                k_.act(mv[:, 2:3], mv[:, 2:3], AF.Exp, [Bmv], [Bmv], scale=-0.5)
                k_.ts(vg[:], vg[:], mv[:, 0:1], mv[:, 2:3], ALU.subtract, ALU.mult, [Bvg, Bmv], [Bvg])
                k_.tt(vg[:], vg[:], rb[:, 0, :], ALU.mult, [Bvg, Bpar], [Bvg])
                vn, Bvn = b16r.next()
                k_.tt(vn[:], vg[:], rb[:, 1, :], ALU.add, [Bvg, Bpar], [Bvn])
                bm, Bbm = bank.next()
                for g in range(8):
                    k_.mm(bm[:, g * 64:(g + 1) * 64], wsT[:, g, :], vn[:, g * 64:(g + 1) * 64], True, True,
                          [Bpar, Bvn], [Bbm])
                t1, Bt1 = tmpr.next()
                k_.tt(t1[:], bm[:], rb[:, 2, :], ALU.add, [Bbm, Bpar], [Bt1])
                ysg, Bysg = b16r.next()
                k_.tt(ysg[:], t1[:], ug[:], ALU.mult, [Bt1, Bug], [Bysg])
                pt, Bpt = pstr.next()
                for cc in range(4):
                    k_.tr(pt[:, cc * 128:(cc + 1) * 128], ysg[:, cc * 128:(cc + 1) * 128], ident[:],
                          [Bysg, Bid], [Bpt])
                for cc in range(4):
                    k_.cp(sgT[:, cc, j * 128:(j + 1) * 128], pt[:, cc * 128:(cc + 1) * 128], [Bpt], [BsgT])

            nxt = sg_a(0)
            for j in range(nb):
                cur = nxt
                if j + 1 < nb:
                    nxt = sg_a(j + 1)
                sg_b(j, *cur)
            WB.release(kuu)
            WB.release(kuv)
            if stop == 'sg':
                return
            brs = [(at_sb, lambda kc: [Bat]), (conf, lambda kc: [Bconf[kc]]), (sgT, lambda kc: [BsgT])]
            for cg in range(4):
                for i in range(3):
                    kg, wg_, Bwg = WB.acquire()
                    ks, wsm, Bws = WS.acquire()
                    src, sbuf = brs[i]
                    for oc in range(4):
                        c = cg * 4 + oc
                        bg, Bbg = fm_group(wg_, Bwg, oc, 16, lambda kc: h_sb[:, kc, 0:T], hbuf, T)
                        g_, Bg = tmpr.next()
                        k_.act(g_[:, 0:T], bg[:, 0:T], AF.Sigmoid, [Bbg, Bpar], [Bg],
                               bias=pp[:, PP_BG + i * 16 + c:PP_BG + i * 16 + c + 1])
                        by, Bby = fm_group(wsm, Bws, oc, 4, lambda kc, src=src: src[:, kc, 0:T], sbuf, T)
                        m = f32b[:, oc, 0:T]
                        if i == 0:
                            k_.tt(m, g_[:, 0:T], by[:, 0:T], ALU.mult, [Bg, Bby], [Bf32[oc]])
                        else:
                            k_.tt(g_[:, 0:T], g_[:, 0:T], by[:, 0:T], ALU.mult, [Bg, Bby], [Bg])
                            if i == 1:
                                k_.tt(m, m, g_[:, 0:T], ALU.add, [Bg, Bf32[oc]], [Bf32[oc]])
                            else:
                                k_.tt(big[:, c, 0:T], m, g_[:, 0:T], ALU.add, [Bg, Bf32[oc]], [Bbig[c]])
                    WB.release(kg)
                    WS.release(ks)
            if stop == 'gates':
                return
            for cg in range(4):
                kw, wt, Bw = WB.acquire()
                gtt, Bgt = gtr.next()
                k_.dma("sp", gtt[:], d["gt1"][:, cg * 512:(cg + 1) * 512], gtr.sem, (), [Bgt])
                for j in range(nb):
                    bk, Bb = tm_group(wt, Bw, j, 16, lambda kc: big[:, kc, :], lambda kc: [Bbig[kc]])
                    resid_add(bk, Bb, j, cg, gtt, Bgt)
                WB.release(kw)
            if stop == 'wo':
                return
            norm_to_h(nb, a2, b2)
            for kh in range(2):
                for u in range(4):
                    kg, wg_, Bwg = WB.acquire()
                    if not ish:
                        kv, wv_, Bwv_ = WB.acquire()
                    for oc in range(4):
                        hc = (kh * 4 + u) * 4 + oc
                        ac = u * 4 + oc
                        bg, Bbg = fm_group(wg_, Bwg, oc, 16, lambda kc: h_sb[:, kc, 0:T], hbuf, T)
                        gp, Bgp = gpr.next()
                        k_.act(gp[:, 2:2 + T], bg[:, 0:T], AF.Copy, [Bbg], [Bgp])
                        k_.cp(gp[:, 0:2], fh[:, hc, :], [Bfh[hc]], [Bgp])
                        cw = PP_FFW + hc * 3
                        acc_, Bacc_ = tmpr.next()
                        k_.ts(acc_[:, 0:T], gp[:, 0:T], pp[:, cw:cw + 1], pp[:, PP_FFB + hc:PP_FFB + hc + 1],
                              ALU.mult, ALU.add, [Bgp, Bpar], [Bacc_])
                        k_.stt(acc_[:, 0:T], gp[:, 1:1 + T], pp[:, cw + 1:cw + 2], acc_[:, 0:T], ALU.mult, ALU.add,
                               [Bgp, Bpar, Bacc_], [Bacc_])
                        k_.stt(acc_[:, 0:T], gp[:, 2:2 + T], pp[:, cw + 2:cw + 3], acc_[:, 0:T], ALU.mult, ALU.add,
                               [Bgp, Bpar, Bacc_], [Bacc_])
                        if ish:
                            k_.ts(fh[:, hc, :], gp[:, T:T + 2], hm[:, 0:1], None, ALU.mult, None,
                                  [Bgp, Bpar], [Bfh[hc]])
                        else:
                            k_.cp(fh[:, hc, :], gp[:, T:T + 2], [Bgp], [Bfh[hc]])
                        if ish:
                            continue
                        k_.act(acc_[:, 0:T], acc_[:, 0:T], AF.Silu, [Bacc_], [Bacc_])
                        bv, Bbv = fm_group(wv_, Bwv_, oc, 16, lambda kc: h_sb[:, kc, 0:T], hbuf, T)
                        k_.tt(big[:, ac, 0:T], acc_[:, 0:T], bv[:, 0:T], ALU.mult, [Bacc_, Bbv], [Bbig[ac]])
                    WB.release(kg)
                    if not ish:
                        WB.release(kv)
                for cg in range(4):
                    if ish:
                        break
                    kw, wt, Bw = WB.acquire()
                    gtt, Bgt = gtr.next()
                    k_.dma("sp", gtt[:], d["gt2"][:, cg * 512:(cg + 1) * 512], gtr.sem, (), [Bgt])
                    for j in range(nb):
                        bk, Bb = tm_group(wt, Bw, j, 16, lambda kc: big[:, kc, :], lambda kc: [Bbig[kc]])
                        resid_add(bk, Bb, j, cg, gtt, Bgt)
                    WB.release(kw)
            if stop == 'ffn':
                return
            if not fin and not ish:
                for j in range(nb):
                    P.out_toks.append(
                        k_.dma("sp", d["xout"][tok0 + j * 128:tok0 + (j + 1) * 128, :], x_sb[:, j, :],
                               P.dsem("xo%d" % j), [Bx[j]], ()))
        if fin and not ish:
            ss, Bss = smallr.next()
            for j in range(nb):
                xs, Bxs = xsr.next()
                k_.act(xs[:], x_sb[:, j, :], AF.Square, [Bx[j]], [Bxs, Bss], accum=ss[:, j:j + 1])
            r, Br = rstd_of(ss[:, 0:nb], nb, 1.0 / D, [Bss])
            for j in range(nb):
                for q4 in range(4):
                    o_ = f32b[:, q4, :]
                    gtt, Bgt = gtr.next()
                    k_.dma("sp", gtt[:], d["gfin"][:, q4 * 512:(q4 + 1) * 512], gtr.sem, (), [Bgt])
                    k_.stt(o_, x_sb[:, j, q4 * 512:(q4 + 1) * 512], r[:, j:j + 1], gtt[:],
                           ALU.mult, ALU.mult, [Bx[j], Br, Bgt], [Bf32[q4]])
                otok = tok0 - (128 if halo else 0) + j * 128
                P.out_toks.append(
                    k_.dma("sp", d["out"][otok:otok + 128, :], f32b[:].rearrange("p a b -> p (a b)"), P.dsem("fo"),
                           Bf32, ()))
        if qkv and not ish:
            norm_to_h(nb, a1n, b1n)
            qtok = tok0 - (128 if halo else 0)
            for which in range(2):
                kw, wt, Bw = WB.acquire()
                dst = d["qT"] if which == 0 else d["kT"]
                for oc in range(4):
                    bk, Bb = fm_group(wt, Bw, oc, 16, lambda kc: h_sb[:, kc, 0:T], hbuf, T)
                    s_, Bs_ = stg.next()
                    if which == 0:
                        k_.act(s_[:, 0:T], bk[:, 0:T], AF.Copy, [Bb], [Bs_], scale=float(1.0 / np.sqrt(DH)))
                    else:
                        k_.cp(s_[:, 0:T], bk[:, 0:T], [Bb], [Bs_])
                    P.out_toks.append(k_.dma("sp", dst[oc, :, qtok:qtok + T], s_[:, 0:T], stg.sem, [Bs_], ()))
                WB.release(kw)
            kw, wt, Bw = WB.acquire()
            for j in range(nb):
                bk, Bb = tm_group(wt, Bw, j, 16, hfn, hbuf)
                s_, Bs_ = stg.next()
                k_.cp(s_[:], bk[:], [Bb], [Bs_])
                if "v_hpnd" in d:
                    nblk_ = qtok // 128 + j
                    P.out_toks.append(k_.dma("sp", d["v_hpnd"][:, :, nblk_, :].rearrange("h p d -> p h d"),
                                             s_[:].rearrange("p (h d) -> p h d", h=4), stg.sem, [Bs_], ()))
                else:
                    P.out_toks.append(k_.dma("sp", d["v"][qtok + j * 128:qtok + (j + 1) * 128, :], s_[:], stg.sem, [Bs_], ()))
            WB.release(kw)


def build_P(pre, qkv, fin, ntile=8, halo=True, stop=None):
    nc = bass.Bass("TRN2", target_bir_lowering=False)
    ntok = ntile * 512 + (128 if halo else 0)
    nown = ntile * 512
    d = {}

    def inp(name, shape, dt=F32):
        d[name] = nc.dram_tensor(name, list(shape), dt, kind="ExternalInput").ap()

    def outp(name, shape, dt=F32):
        d[name] = nc.dram_tensor(name, list(shape), dt, kind="ExternalOutput").ap()

    inp("xin", [ntok, D])
    inp("hmask", [128, 1])
    if pre:
        inp("pp", [128, NPP])
        inp("modpp", [128, 96])
        inp("rb", [128, 3, 512])
        inp("wsT", [128, 8, 128])
        inp("gt1", [128, D])
        inp("gt2", [128, D])
        inp("attnT", [4, 128, ntok], BF16)
        inp("w_in", [D, 3584])
        inp("w_gate", [D, 3 * D])
        inp("sbo", [512, D])
        inp("cvo", [512, D])
        inp("sgo", [512, D])
        inp("w_o", [D, D])
        inp("w_up", [D, 4 * D])
        inp("w_dn", [2 * D, D])
        if not fin:
            outp("xout", [ntok, D])
    if qkv:
        inp("g1n", [128, 16])
        inp("modn", [128, 32])
        inp("w_in_n", [D, 3584])
        outp("qT", [4, 128, nown], BF16)
        outp("kT", [4, 128, nown], BF16)
        outp("v", [nown, 512], BF16)
    if fin:
        inp("gfin", [128, D])
        outp("out", [nown, D])
    with ExitStack() as st:
        P = Prog(nc, st)
        emit_P(nc, P, d, pre, qkv, fin, ntile=ntile, halo=halo, stop=stop)
        P.finish()
        P.emit()
    return nc


def emit_mod(nc, P, d, ncc=96, collective=None):
    k_ = K(P)
    nun = ncc // 4
    Bpar = Buf()
    dc = P.dsem("mc")
    cpp = P.sb("m_cpp", [128, 16], F32)
    bapp = P.sb("m_bapp", [128, L_, ncc], F32)
    k_.dma("sp", cpp[:], d["cpp"][:, :], dc, (), [Bpar])
    k_.dma("sp", bapp[:], d["bapp"][:, :, :], dc, (), [Bpar])
    cond = P.sb("m_cond", [128, 16, 2], BF16)
    Bc = Buf()
    for r in range(2):
        k_.act(cond[:, :, r], cpp[:], AF.Silu, [Bpar], [Bc])
    units = []
    for l in range(L_):
        for u in range(nun):
            units.append(wunit(d["w_ada"][l], 0, 16, u * 512))
    WB = WRing(P, "mw", 3, [128, 16, 512], units)
    pm = Rot(P, "m_ps", 2, [128, ncc, 2], F32, psum=True)
    res = P.sb("m_res", [128, L_, 96], F32)
    resp = P.sb("m_resp", [128, L_, ncc], F32)
    Bres = Buf()
    for l in range(L_):
        ps, Bps = pm.next()
        for u in range(nun):
            ku, wt, Bw = WB.acquire()
            for oc in range(4):
                j = u * 4 + oc
                for kc in range(16):
                    k_.mm(ps[:, j, :], wt[:, kc, oc * 128:(oc + 1) * 128], cond[:, kc, :], kc == 0, kc == 15,
                          [Bw, Bc], [Bps])
            WB.release(ku)
        k_.tt((res if collective is None else resp)[:, l, :], ps[:, :, 0], bapp[:, l, :], ALU.add,
              [Bps, Bpar], [Bres])
    if collective is not None:
        MP, MG = d["MP"], d["MG"]
        k_.dma("sp", MP[:, :], resp[:].rearrange("p l c -> p (l c)"), P.dsem("mp"), [Bres], ())
        P.barrier()
        collective(MP[:, :], MG[:, :])
        P.barrier()
        for s4 in range(4):
            k_.dma("sp", res[:, :, s4 * ncc:(s4 + 1) * ncc],
                   MG[s4 * 128:(s4 + 1) * 128, :].rearrange("p (l c) -> p l c", l=L_), P.dsem("mg"), (), [Bres])
    P.out_toks.append(k_.dma("sp", d["modpp"][:, :, :], res[:], P.dsem("mo"), [Bres], ()))
    if "gtb" in d:
        onesf = P.sb("m_onesf", [128, 128], F32)
        identf = P.sb("m_identf", [128, 128], F32)
        Bk = Buf()
        k_.memset(onesf[:], 1.0, [Bk])
        P.op("pool", lambda e: e.affine_select(out=identf[:], in_=onesf[:], pattern=[[1, 128]],
                                               compare_op=ALU.is_equal, fill=0.0, base=0, channel_multiplier=-1),
             [Bk], [Bk])
        lr = Rot(P, "m_lh", 2, [128, 128], F32)
        gr = Rot(P, "m_gt", 2, [128, D], F32)
        pg = Rot(P, "m_pg", 2, [128, 512], F32, psum=True)
        for l in range(L_):
            for kk, sec in enumerate((2, 5)):
                g_, Bg = gr.next()
                for c4 in range(4):
                    pb, Bpb = pg.next()
                    for cc in range(4):
                        c = c4 * 4 + cc
                        lh, Blh = lr.next()
                        k_.ts(lh[:], onesf[:], res[:, l, sec * 16 + c:sec * 16 + c + 1], None, ALU.mult, None,
                              [Bk, Bres], [Blh])
                        k_.mm(pb[:, cc * 128:(cc + 1) * 128], lh[:], identf[:], True, True, [Blh, Bk], [Bpb])
                    k_.cp(g_[:, c4 * 512:(c4 + 1) * 512], pb[:], [Bpb], [Bg])
                P.out_toks.append(k_.dma("sp", d["gtb"][l, kk, :, :], g_[:], gr.sem, [Bg], ()))


def build_mod():
    nc = bass.Bass("TRN2", target_bir_lowering=False)
    d = {}
    d["cpp"] = nc.dram_tensor("cpp", [128, 16], F32, kind="ExternalInput").ap()
    d["bapp"] = nc.dram_tensor("bapp", [128, L_, 96], F32, kind="ExternalInput").ap()
    d["w_ada"] = nc.dram_tensor("w_ada", [L_, D, 6 * D], F32, kind="ExternalInput").ap()
    d["modpp"] = nc.dram_tensor("modpp", [128, L_, 96], F32, kind="ExternalOutput").ap()
    with ExitStack() as st:
        P = Prog(nc, st)
        emit_mod(nc, P, d)
        P.finish()
        P.emit()
    return nc


def pp_of(vec, n):
    return np.ascontiguousarray(np.asarray(vec, np.float32).reshape(n, 128).T)


def layer_inputs(W, l, modpp_l):
    pp = np.zeros((128, NPP), np.float32)
    pp[:, PP_G1:PP_G1 + 16] = pp_of(W["g_norm1"][l], 16)
    pp[:, PP_G2:PP_G2 + 16] = pp_of(W["g_norm2"][l], 16)
    pp[:, PP_BG:PP_BG + 48] = pp_of(W["b_gate"][l], 48)
    pp[:, PP_CVW:PP_CVW + 124] = W["cv_w_dw"][l].T.reshape(4, 128, 31).transpose(1, 0, 2).reshape(128, 124)
    pp[:, PP_CVB:PP_CVB + 4] = pp_of(W["cv_b_dw"][l], 4)
    pp[:, PP_CVG:PP_CVG + 4] = pp_of(W["cv_ln_g"][l], 4)
    pp[:, PP_CVBT:PP_CVBT + 4] = pp_of(W["cv_ln_b"][l], 4)
    pp[:, PP_FFW:PP_FFW + 96] = W["ffn_w_dw"][l].T.reshape(32, 128, 3).transpose(1, 0, 2).reshape(128, 96)
    pp[:, PP_FFB:PP_FFB + 32] = pp_of(W["ffn_b_dw"][l], 32)
    rb = np.zeros((128, 3, 512), np.float32)
    rb[:, 0, :] = W["sg_ln_g"][l][None, :]
    rb[:, 1, :] = W["sg_ln_b"][l][None, :]
    rb[:, 2, :] = np.repeat(W["sg_b_s"][l].T, 64, axis=1)
    wsT = np.ascontiguousarray(W["sg_w_s"][l].transpose(2, 0, 1))
    gt1 = np.ascontiguousarray(np.broadcast_to(modpp_l[:, 32:48].T.reshape(1, D), (128, D)))
    gt2 = np.ascontiguousarray(np.broadcast_to(modpp_l[:, 80:96].T.reshape(1, D), (128, D)))
    return dict(pp=pp, modpp=np.ascontiguousarray(modpp_l), rb=rb, wsT=wsT, gt1=gt1, gt2=gt2,
                w_in=W["w_in"][l], w_gate=W["w_gate"][l], sbo=W["sb_w_out"][l], cvo=W["cv_w_out"][l],
                sgo=W["sg_w_out"][l], w_o=W["w_o"][l], w_up=W["ffn_w_up"][l], w_dn=W["ffn_w_down"][l])


def with_halo(a, t0, n, axis=0):
    if t0 >= 128:
        sl = [slice(None)] * a.ndim
        sl[axis] = slice(t0 - 128, t0 + n)
        return np.ascontiguousarray(a[tuple(sl)])
    shp = list(a.shape)
    shp[axis] = 128 + n
    out = np.zeros(shp, a.dtype)
    sl_o = [slice(None)] * a.ndim
    sl_o[axis] = slice(128, 128 + n)
    sl_i = [slice(None)] * a.ndim
    sl_i[axis] = slice(0, n)
    out[tuple(sl_o)] = a[tuple(sl_i)]
    return out


def run_attn(nc_attn, q_parts, k_parts, v_parts):
    in_maps = []
    for b in range(NB_):
        qT = np.concatenate([q_parts[b * 4 + s] for s in range(4)], axis=2)
        kT = np.concatenate([k_parts[b * 4 + s] for s in range(4)], axis=2)
        v = np.concatenate([v_parts[b * 4 + s] for s in range(4)], axis=0)
        for h in range(NH):
            in_maps.append({"qT": np.ascontiguousarray(qT[h]), "kT": np.ascontiguousarray(kT[h]),
                            "v": np.ascontiguousarray(v[:, h * 128:(h + 1) * 128])})
    res = run_bass_kernel_spmd(nc_attn, in_maps, core_ids=list(range(NCORE)))
    return [[np.asarray(res.results[b * 4 + h]["oT"]) for h in range(NH)] for b in range(NB_)]


def kernel_unfused(**inputs):
    W = {k: np.asarray(v) for k, v in inputs.items()}
    x = W["x"]
    c = W["c"]
    cores = [(b, s) for b in range(NB_) for s in range(4)]
    nc_m = build_mod()
    bapp = np.ascontiguousarray(np.stack([pp_of(W["b_ada"][l], 96) for l in range(L_)], axis=1))
    res = run_bass_kernel_spmd(nc_m, [{"cpp": pp_of(c[b], 16), "bapp": bapp, "w_ada": W["w_ada"]}
                                      for b in range(NB_)], core_ids=list(range(NB_)))
    modpp = [np.asarray(res.results[b]["modpp"]) for b in range(NB_)]
    ones_hm = np.ones((128, 1), np.float32)
    zeros_hm = np.zeros((128, 1), np.float32)
    nc_p0 = build_P(False, True, False, halo=False)
    in_maps = []
    for (b, s) in cores:
        in_maps.append({"xin": np.ascontiguousarray(x[b, s * TOKC:(s + 1) * TOKC]), "hmask": ones_hm,
                        "g1n": pp_of(W["g_norm1"][0], 16),
                        "modn": np.ascontiguousarray(modpp[b][:, 0, 0:32]), "w_in_n": W["w_in"][0]})
    res = run_bass_kernel_spmd(nc_p0, in_maps, core_ids=list(range(NCORE)))
    qp = [np.asarray(r["qT"]) for r in res.results]
    kp = [np.asarray(r["kT"]) for r in res.results]
    vp = [np.asarray(r["v"]) for r in res.results]
    nc_attn = build_attn()
    xcur = x
    out = None
    for l in range(L_):
        oT = run_attn(nc_attn, qp, kp, vp)
        last = (l == L_ - 1)
        nc_p = build_P(True, not last, last)
        in_maps = []
        for (b, s) in cores:
            t0 = s * TOKC
            m = layer_inputs(W, l, modpp[b][:, l, :])
            m["xin"] = with_halo(xcur[b], t0, TOKC, axis=0)
            m["hmask"] = zeros_hm if s == 0 else ones_hm
            m["attnT"] = np.stack([with_halo(oT[b][h], t0, TOKC, axis=1) for h in range(NH)], axis=0)
            if not last:
                m["g1n"] = pp_of(W["g_norm1"][l + 1], 16)
                m["modn"] = np.ascontiguousarray(modpp[b][:, l + 1, 0:32])
                m["w_in_n"] = W["w_in"][l + 1]
            else:
                m["gfin"] = np.ascontiguousarray(np.broadcast_to(W["g_final"][None, :], (128, D)))
            in_maps.append(m)
        res = run_bass_kernel_spmd(nc_p, in_maps, core_ids=list(range(NCORE)))
        if not last:
            qp = [np.asarray(r["qT"]) for r in res.results]
            kp = [np.asarray(r["kT"]) for r in res.results]
            vp = [np.asarray(r["v"]) for r in res.results]
            xn = np.empty_like(x)
            for i, (b, s) in enumerate(cores):
                xn[b, s * TOKC:(s + 1) * TOKC] = np.asarray(res.results[i]["xout"])[128:]
            xcur = xn
        else:
            out = np.empty_like(x)
            for i, (b, s) in enumerate(cores):
                out[b, s * TOKC:(s + 1) * TOKC] = np.asarray(res.results[i]["out"])
    return out


U32 = mybir.dt.uint32
LAYER_W = (("w_in", [D, 3584]), ("w_gate", [D, 3 * D]), ("sbo", [512, D]), ("cvo", [512, D]), ("sgo", [512, D]),
           ("w_o", [D, D]), ("w_up", [D, 4 * D]), ("w_dn", [2 * D, D]))
GROUPS = [[0, 1, 2, 3], [4, 5, 6, 7]]


def build_fused():
    nc = bass.Bass("TRN2", target_bir_lowering=False)
    NT = HALO + TOKC
    E = {}

    def inp(name, shape, dt=F32):
        E[name] = nc.dram_tensor(name, list(shape), dt, kind="ExternalInput").ap()

    def internal(name, shape, dt):
        return nc.dram_tensor(name, list(shape), dt, kind="Internal").ap()

    inp("xin", [NT, D])
    inp("hmask", [128, 1])
    inp("cpp", [128, 16])
    inp("bapp", [128, L_, 24])
    inp("idx", [128, 32], U32)
    inp("w_ada", [L_, D, 6 * D // 4])
    inp("gfin", [128, D])
    for l in range(L_):
        inp("pp%d" % l, [128, NPP])
        inp("rb%d" % l, [128, 3, 512])
        inp("wsT%d" % l, [128, 8, 128])
        for nm, shp in LAYER_W:
            inp("%s%d" % (nm, l), shp)
    out = nc.dram_tensor("out", [TOKC, D], F32, kind="ExternalOutput").ap()
    modpp = internal("modpp_i", [128, L_, 96], F32)
    gtb = internal("gtb_i", [L_, 2, 128, D], F32)
    SQ = [internal("SQ%d" % l, [4, 3, 128, 4096], BF16) for l in range(L_)]
    G = [internal("G%d" % l, [4 * 3 * 4 * 128, 4096], BF16) for l in range(L_)]
    SA = [internal("SA%d" % l, [4, 128, 4096], BF16) for l in range(L_)]
    SH = [internal("SH%d" % l, [4, 128, 128], BF16) for l in range(L_)]
    A = [internal("A%d" % l, [4 * 4 * 128, 4096], BF16) for l in range(L_)]
    AH = [internal("AH%d" % l, [4 * 4 * 128, 128], BF16) for l in range(L_)]
    ATL = [internal("ATL%d" % l, [4, 128, NT], BF16) for l in range(L_)]
    WBS = [internal("WBS%d" % l, [53, 128, 16, 512], BF16) for l in range(L_)]
    WSS = [internal("WSS%d" % l, [12, 128, 4, 512], BF16) for l in range(L_)]
    MP = internal("MP", [128, L_ * 24], F32)
    MG = internal("MG", [4 * 128, L_ * 24], F32)
    X1 = internal("X1", [NT, D], F32)
    XH = internal("XH", [4 * 128, D], F32)

    def layer_d(l):
        dd = {"pp": E["pp%d" % l], "modpp": modpp[:, l, :], "rb": E["rb%d" % l], "wsT": E["wsT%d" % l],
              "gt1": gtb[l, 0], "gt2": gtb[l, 1], "attnT": ATL[l], "hmask": E["hmask"]}
        for nm, _ in LAYER_W:
            dd[nm] = E["%s%d" % (nm, l)]
        return dd

    def qkv_d(l):
        return {"g1n": E["pp%d" % l][:, PP_G1:PP_G1 + 16], "modn": modpp[:, l, 0:32], "w_in_n": E["w_in%d" % l],
                "qT": SQ[l][:, 0], "kT": SQ[l][:, 1],
                "v_hpnd": SQ[l][:, 2].rearrange("h p (n d) -> h p n d", d=128), "hmask": E["hmask"]}

    with ExitStack() as gs:
        P = Prog(nc, gs)
        k_ = K(P)
        ccB = Buf()

        def collective(src, dst):
            P.op("pool", lambda e: e.collective_compute("AllGather", ALU.bypass, replica_groups=GROUPS,
                                                        ins=[src], outs=[dst]),
                 (), [ccB], dma=P.dsem("cc"), dinc=1)

        def phase(name):
            st = ExitStack()
            P.begin_phase(name + "_", st)
            return st

        with phase("m"):
            emit_mod(nc, P, {"cpp": E["cpp"], "bapp": E["bapp"], "w_ada": E["w_ada"], "modpp": modpp, "gtb": gtb,
                             "MP": MP, "MG": MG}, ncc=24, collective=collective)
            P.end_phase()
        with phase("p0"):
            d0 = qkv_d(0)
            d0["xin"] = E["xin"][HALO:NT, :]
            emit_P(nc, P, d0, False, True, False, halo=False)
            P.end_phase()
        for l in range(L_):
            last = (l == L_ - 1)
            with phase("xq%d" % l):
                for h in range(4):
                    for which in range(3):
                        r0 = ((h * 3 + which) * 4) * 128
                        collective(SQ[l][h, which], G[l][r0:r0 + 512, :])
                P.end_phase()
            with phase("b%d" % l):
                idx_sb = P.sb("idx", [128, 32], U32)
                Bidx = Buf()
                k_.dma("sp", idx_sb[:], E["idx"][:, :], P.dsem("idx"), (), [Bidx])
                cd = layer_d(l)
                if l + 1 < L_:
                    cd["w_in_n"] = E["w_in%d" % (l + 1)]
                cb, cs = tile_units(cd, True, l + 1 < L_)
                side = convert_gen(P, cb, cs, WBS[l], WSS[l])
                emit_attn(nc, P, None, None, None, None, gather=(G[l], idx_sb, Bidx), fused_out=(SA[l], SH[l]),
                          side=side, side_every=max(1, 1000 // (len(cb) + len(cs) + 4)))
                P.end_phase()
            with phase("xa%d" % l):
                for dest in range(4):
                    collective(SA[l][dest], A[l][dest * 512:(dest + 1) * 512, :])
                collective(SH[l].rearrange("s p c -> (s p) c"), AH[l][:, :])
                P.barrier()
                idx_sb = P.sb("idx", [128, 32], U32)
                Bidx = Buf()
                k_.dma("sp", idx_sb[:], E["idx"][:, :], P.dsem("idx"), (), [Bidx])
                bnc = Rot(P, "bnc", 2, [128, 4096], BF16)
                bnh = Rot(P, "bnh", 2, [128, 128], BF16)
                for h in range(4):
                    t, Bt = bnc.next()
                    P.op("pool", lambda e, t=t, h=h: e.indirect_dma_start(
                        out=t[:], out_offset=None, in_=A[l][:, :],
                        in_offset=IndirectOffsetOnAxis(ap=idx_sb[:, 12 + h:13 + h], axis=0)),
                        [Bidx], [Bt], dma=bnc.sem)
                    k_.dma("sp", ATL[l][h, :, HALO:NT], t[:], P.dsem("atl"), [Bt], ())
                    t2, Bt2 = bnh.next()
                    P.op("pool", lambda e, t2=t2, h=h: e.indirect_dma_start(
                        out=t2[:], out_offset=None, in_=AH[l][:, :],
                        in_offset=IndirectOffsetOnAxis(ap=idx_sb[:, 16 + h:17 + h], axis=0)),
                        [Bidx], [Bt2], dma=bnh.sem)
                    k_.dma("sp", ATL[l][h, :, 0:HALO], t2[:], P.dsem("atl"), [Bt2], ())
                if l > 0:
                    collective(X1[TOKC:NT, :], XH[:, :])
                    P.barrier()
                    xb = P.sb("xbnc", [128, D], F32)
                    Bxb = Buf()
                    P.op("pool", lambda e: e.indirect_dma_start(
                        out=xb[:], out_offset=None, in_=XH[:, :],
                        in_offset=IndirectOffsetOnAxis(ap=idx_sb[:, 20:21], axis=0)),
                        [Bidx], [Bxb], dma=P.dsem("xbnc"))
                    k_.dma("sp", X1[0:HALO, :], xb[:], P.dsem("xh"), [Bxb], ())
                P.end_phase()
            with phase("p%d" % (l + 1)):
                dd = layer_d(l)
                dd["xin"] = E["xin"] if l == 0 else X1
                dd["Wb"] = WBS[l]
                dd["Wsb"] = WSS[l]
                if not last:
                    dd.update(qkv_d(l + 1))
                    dd["xout"] = X1
                else:
                    dd["gfin"] = E["gfin"]
                    dd["out"] = out
                emit_P(nc, P, dd, True, not last, last)
                if last:
                    P.finish()
                P.end_phase()
    return nc


def core_idx_table(s):
    p = np.arange(128, dtype=np.int64)
    t = np.zeros((128, 32), np.int64)
    h = s
    for which in range(3):
        for src in range(4):
            t[:, which * 4 + src] = ((h * 3 + which) * 4 + src) * 128 + p
    for hh in range(4):
        t[:, 12 + hh] = (s * 4 + hh) * 128 + p
        t[:, 16 + hh] = (hh * 4 + s) * 128 + p
    t[:, 20] = max(s - 1, 0) * 128 + p
    return t.astype(np.uint32)


def kernel_fused(**inputs):
    W = {k: np.asarray(v) for k, v in inputs.items()}
    x = W["x"]
    nc = build_fused()
    bapp = np.ascontiguousarray(np.stack([pp_of(W["b_ada"][l], 96) for l in range(L_)], axis=1))
    bapp_s = [np.ascontiguousarray(bapp[:, :, s * 24:(s + 1) * 24]) for s in range(4)]
    wada_s = [np.ascontiguousarray(W["w_ada"][:, :, s * 3072:(s + 1) * 3072]) for s in range(4)]
    gfin = np.ascontiguousarray(np.broadcast_to(W["g_final"][None, :], (128, D)))
    lay = []
    dummy = np.zeros((128, 96), np.float32)
    for l in range(L_):
        m = layer_inputs(W, l, dummy)
        lay.append(m)
    in_maps = []
    for b in range(NB_):
        for s in range(4):
            m = {"xin": with_halo(x[b], s * TOKC, TOKC, axis=0),
                 "hmask": (np.zeros if s == 0 else np.ones)((128, 1), np.float32),
                 "cpp": pp_of(W["c"][b], 16), "bapp": bapp_s[s], "idx": core_idx_table(s), "w_ada": wada_s[s],
                 "gfin": gfin}
            for l in range(L_):
                m["pp%d" % l] = lay[l]["pp"]
                m["rb%d" % l] = lay[l]["rb"]
                m["wsT%d" % l] = lay[l]["wsT"]
                for nm, _ in LAYER_W:
                    m["%s%d" % (nm, l)] = lay[l][nm]
            in_maps.append(m)
    res = run_bass_kernel_spmd(nc, in_maps, core_ids=list(range(NCORE)))
    out = np.empty_like(x)
    for i in range(NCORE):
        b, s = divmod(i, 4)
        out[b, s * TOKC:(s + 1) * TOKC] = np.asarray(res.results[i]["out"])
    return out


def kernel(**inputs):
    return kernel_fused(**inputs)
```

```python
import numpy as np
import ml_dtypes
import concourse.bass as bass
import concourse.mybir as mybir
from concourse.bass_utils import run_bass_kernel_spmd
from contextlib import ExitStack
from concourse.bass import IndirectOffsetOnAxis

F32 = mybir.dt.float32
BF16 = mybir.dt.bfloat16
AF = mybir.ActivationFunctionType
ALU = mybir.AluOpType

D = 2048
S = 16384
NB_ = 2
L_ = 2
NH = 4
DH = 128
EPS = 1e-6
NCORE = 8
TOKC = 4096
HALO = 128
ENGS = ("pe", "act", "dve", "pool", "sp")


class Buf:
    __slots__ = ("name", "lw", "rd", "x")

    def __init__(self, name="", x=False):
        self.name = name
        self.lw = None
        self.rd = {}
        self.x = x


class Prog:
    def __init__(self, nc, stack):
        self.nc = nc
        self.stack = stack
        self.ops = {e: [] for e in ENGS}
        self.cnt = {e: 0 for e in ENGS}
        self.ecnt = {e: 0 for e in ENGS}
        self.posof = {e: {} for e in ENGS}
        self.seen = {e: {} for e in ENGS}
        self.sems = {}
        self.dcnt = {}
        for e in ENGS:
            self.sems[e] = stack.enter_context(nc.semaphore("s_" + e))
        self.same_dist = 10 ** 9
        self.out_toks = []
        self.gstack = stack
        self.prefix = ""

    def begin_phase(self, prefix, stack):
        self.prefix = prefix
        self.stack = stack

    def barrier(self):
        for e in ENGS:
            waits = []
            for e2 in ENGS:
                if e2 != e and self.ecnt[e2] > 0:
                    self._need(e, (e2, self.ecnt[e2]), waits)
            for dname, v in self.dcnt.items():
                if v > 0:
                    self._need(e, (dname, v), waits)
            self.ops[e].append((None, waits, None, False))

    def end_phase(self):
        self.barrier()
        self.emit()
        self.ops = {e: [] for e in ENGS}

    def dsem(self, name):
        if name not in self.sems:
            self.sems[name] = self.gstack.enter_context(self.nc.semaphore("d_" + name))
            self.dcnt[name] = 0
        return name

    def sb(self, name, shape, dt):
        return self.stack.enter_context(self.nc.sbuf_tensor("sb_" + self.prefix + name, list(shape), dt))

    def ps(self, name, shape, dt=F32):
        return self.stack.enter_context(self.nc.psum_tensor("ps_" + self.prefix + name, list(shape), dt))

    def _need(self, e, tok, waits):
        if tok is None:
            return
        k, v = tok
        if k == e:
            if e == "pe":
                return
            if self.cnt[e] - self.posof[e][v] >= self.same_dist:
                return
        if self.seen[e].get(k, 0) >= v:
            return
        self.seen[e][k] = v
        waits.append((k, v))

    def op(self, e, fn, reads=(), writes=(), dma=None, dinc=16, embed=False):
        if any(b.x for b in reads):
            writes = list(writes) + [b for b in reads if b.x and b not in writes]
            reads = [b for b in reads if not b.x]
        waits = []
        for b in reads:
            self._need(e, b.lw, waits)
        for b in writes:
            self._need(e, b.lw, waits)
            for k, v in b.rd.items():
                self._need(e, (k, v), waits)
        if dma is None:
            self.ecnt[e] += 1
            self.posof[e][self.ecnt[e]] = self.cnt[e]
            tok = (e, self.ecnt[e])
            inc = (e, 1)
        else:
            self.dcnt[dma] += dinc
            tok = (dma, self.dcnt[dma])
            inc = (dma, dinc)
        for b in reads:
            if b.rd.get(tok[0], 0) < tok[1]:
                b.rd[tok[0]] = tok[1]
        for b in writes:
            b.lw = tok
            b.rd = {}
        self.cnt[e] += 1
        self.ops[e].append((fn, waits, inc, embed))
        return tok

    def finish(self, e="sp"):
        waits = []
        for t in self.out_toks:
            self._need(e, t, waits)
        self.ops[e].append((None, waits, None, False))

    def emit(self):
        nc = self.nc
        sems = self.sems
        allops = self.ops
        with nc.Block() as block:
            def run(eng, key):
                for fn, waits, inc, embed in allops[key]:
                    if embed and fn is not None and waits:
                        for k, v in waits[:-1]:
                            eng.wait_ge(sems[k], v)
                        ins = fn(eng)
                        ins._wait_ge(sems[waits[-1][0]], waits[-1][1])
                        ins.then_inc(sems[inc[0]], inc[1])
                        continue
                    for k, v in waits:
                        eng.wait_ge(sems[k], v)
                    if fn is not None:
                        fn(eng).then_inc(sems[inc[0]], inc[1])

            @block.tensor
            def _(eng):
                run(eng, "pe")

            @block.scalar
            def _(eng):
                run(eng, "act")

            @block.vector
            def _(eng):
                run(eng, "dve")

            @block.gpsimd
            def _(eng):
                run(eng, "pool")

            @block.sync
            def _(eng):
                run(eng, "sp")


class Rot:
    def __init__(self, P, name, n, shape, dt, psum=False):
        mk = P.ps if psum else P.sb
        self.t = [mk("%s%d" % (name, i), shape, dt) for i in range(n)]
        self.b = [Buf("%s%d" % (name, i), x=psum) for i in range(n)]
        self.i = 0
        self.P = P
        self.name = name
        self.cur = 0

    def next(self):
        r = (self.t[self.i], self.b[self.i])
        self.cur = self.i
        self.i = (self.i + 1) % len(self.t)
        return r

    @property
    def sem(self):
        return self.P.dsem("%s_%d" % (self.name, self.cur))


class K:
    def __init__(self, P):
        self.P = P

    def mm(self, out, lhsT, rhs, start, stop, reads, writes):
        self.P.op("pe", lambda e: e.matmul(out, lhsT=lhsT, rhs=rhs, start=start, stop=stop),
                  reads, writes)

    def tr(self, out, in_, ident, reads, writes):
        self.P.op("pe", lambda e: e.transpose(out=out, in_=in_, identity=ident), reads, writes)

    def act(self, out, in_, func, reads, writes, bias=None, scale=None, accum=None, eng="act"):
        kw = {}
        if bias is not None:
            kw["bias"] = bias
        if scale is not None:
            kw["scale"] = scale
        if accum is not None:
            kw["accum_out"] = accum
        self.P.op("act", lambda e: e.activation(out=out, in_=in_, func=func, **kw), reads, writes,
                  embed=(accum is None))

    def tt(self, out, in0, in1, op, reads, writes, eng="dve"):
        self.P.op(eng, lambda e: e.tensor_tensor(out=out, in0=in0, in1=in1, op=op), reads, writes, embed=True)

    def ts(self, out, in0, s1, s2, op0, op1, reads, writes, eng="dve"):
        if s2 is None:
            self.P.op(eng, lambda e: e.tensor_scalar(out=out, in0=in0, scalar1=s1, scalar2=None, op0=op0),
                      reads, writes, embed=True)
        else:
            self.P.op(eng, lambda e: e.tensor_scalar(out=out, in0=in0, scalar1=s1, scalar2=s2,
                                                     op0=op0, op1=op1), reads, writes, embed=True)

    def stt(self, out, in0, scalar, in1, op0, op1, reads, writes, eng="dve"):
        self.P.op(eng, lambda e: e.scalar_tensor_tensor(out=out, in0=in0, scalar=scalar, in1=in1,
                                                        op0=op0, op1=op1), reads, writes, embed=True)

    def cp(self, out, in_, reads, writes, eng="dve"):
        self.P.op(eng, lambda e: e.tensor_copy(out=out, in_=in_), reads, writes, embed=True)

    def memset(self, ap, val, writes, eng="dve"):
        self.P.op(eng, lambda e: e.memset(ap, val), (), writes)

    def dma(self, q, out, in_, sem, reads, writes):
        return self.P.op(q, lambda e: e.dma_start(out=out, in_=in_), reads, writes, dma=sem)

    def asel(self, out, in_, n, cmp, base, cm, reads, writes, step=1):
        self.P.op("pool", lambda e: e.affine_select(out=out, in_=in_, pattern=[[step, n]], compare_op=cmp,
                                                    fill=0.0, base=base, channel_multiplier=cm),
                  reads, writes)


def emit_attn(nc, P, qT, kT, v, oT, seq=S, lag=2, gather=None, fused_out=None, nstream=2, side=None, side_every=24, pair=False):
    k_ = K(P)
    nblk = seq // 128
    nqt = seq // 512
    q_sb = P.sb("q_sb", [128, seq], BF16)
    k_sb = P.sb("k_sb", [128, seq], BF16)
    v_sb = P.sb("v_sb", [128, nblk, 128], BF16)
    CH = 2048 if gather is None else 4096
    nch = seq // CH
    qB = [Buf() for _ in range(nch)]
    kB = [Buf() for _ in range(nch)]
    vB = [Buf() for _ in range(nch)]
    if gather is None:
        for c in range(nch):
            k_.dma("sp", q_sb[:, c * CH:(c + 1) * CH], qT[:, c * CH:(c + 1) * CH], P.dsem("aq%d" % c), (), [qB[c]])
            k_.dma("sp", k_sb[:, c * CH:(c + 1) * CH], kT[:, c * CH:(c + 1) * CH], P.dsem("ak%d" % c), (), [kB[c]])
            nb_c = CH // 128
            k_.dma("sp", v_sb[:, c * nb_c:(c + 1) * nb_c, :],
                   v[c * CH:(c + 1) * CH, :].rearrange("(n p) d -> p n d", p=128), P.dsem("av%d" % c), (), [vB[c]])
    else:
        G2d, idx_sb, Bidx = gather
        for src in range(4):
            dsts = (q_sb[:, src * 4096:(src + 1) * 4096], k_sb[:, src * 4096:(src + 1) * 4096],
                    v_sb[:, src * 32:(src + 1) * 32, :].rearrange("p n d -> p (n d)"))
            for which in range(3):
                col = which * 4 + src
                P.op("pool", lambda e, o=dsts[which], col=col: e.indirect_dma_start(
                    out=o, out_offset=None, in_=G2d[:, :],
                    in_offset=IndirectOffsetOnAxis(ap=idx_sb[:, col:col + 1], axis=0)),
                    [Bidx], [(qB, kB, vB)[which][src]], dma=P.dsem("ag%d" % col))
    cf = P.sb("a_cf", [128, 128], F32)
    Bcf = Buf()
    triI = P.sb("a_tri", [128, 128], BF16)
    Btri = Buf()
    nones = P.sb("a_nones", [128, 128], BF16)
    Bno = Buf()
    k_.memset(cf[:], -1.0, [Bcf], eng="pool")
    k_.asel(triI[:], cf[:], 128, ALU.is_ge, 0, 1, [Bcf], [Btri], step=-1)
    k_.cp(nones[:], cf[:], [Bcf], [Bno], eng="pool")

    if fused_out is not None:
        zt = P.sb("a_zero", [128, 128], BF16)
        Bzt = Buf()
        k_.memset(zt[:], 0.0, [Bzt], eng="pool")
        k_.dma("sp", fused_out[1][0, :, :], zt[:], P.dsem("azero"), [Bzt], ())
    if pair:
        _emit_attn_pair(P, k_, locals(), seq, lag, fused_out, oT, side, side_every)
        return
    NSTR = nstream
    zr = Rot(P, "a_z", 8 - NSTR, [128, 512], F32, psum=True)
    accs = [Rot(P, "a_acc%d" % i, 1, [128, 512], F32, psum=True) for i in range(NSTR)]
    er = Rot(P, "a_e", 2 * NSTR, [128, 512], F32)
    lr = Rot(P, "a_l", NSTR * (lag + 2) + 1, [128, 512], BF16)
    wr = Rot(P, "a_w", 3 * NSTR, [128, 512], BF16)
    lcrs = [Rot(P, "a_lc%d" % i, 2, [128, 512], BF16) for i in range(NSTR)]
    osr = Rot(P, "a_os", 2, [128, 512], BF16)

    def stage1(qt, n):
        z, Bz = zr.next()
        e_, Be = er.next()
        l_, Bl = lr.next()
        qs = slice(qt * 512, (qt + 1) * 512)
        k_.mm(z[:], k_sb[:, n * 128:(n + 1) * 128], q_sb[:, qs], True, False,
              [kB[(n * 128) // CH], qB[(qt * 512) // CH]], [Bz])
        k_.act(e_[:], z[:], AF.Exp, [Bz], [Be])
        k_.act(l_[:], e_[:], AF.Ln, [Be], [Bl], bias=1.0)
        r = n - 4 * qt
        if r >= 0:
            k_.asel(l_[:], l_[:], 512, ALU.is_gt, -128 * r, -1, [Bl], [Bl])
        return (z, Bz, l_, Bl, r)

    def tile_gen(qt, sid):
        acc, Bacc = accs[sid].next()
        lcr = lcrs[sid]
        blocks = list(range(4 * qt + 3, -1, -1))
        st = {}
        lc_cur = None
        nsteps = len(blocks) + lag
        pend = None
        for i in range(nsteps):
            if i < len(blocks):
                st[i] = stage1(qt, blocks[i])
            j = i - lag
            if j >= 0:
                n = blocks[j]
                z, Bz, l_, Bl, r = st.pop(j)
                first = (j == 0)
                last = (n == 0)
                k_.mm(z[:], triI[:], l_[:], False, first, [Btri, Bl], [Bz])
                if not first:
                    k_.mm(z[:], nones[:], lc_cur[0][:], False, True, [Bno, lc_cur[1]], [Bz])
                if pend is not None:
                    k_.mm(*pend)
                    pend = None
                w_, Bw = wr.next()
                k_.act(w_[:], z[:], AF.Exp, [Bz], [Bw])
                if r >= 0:
                    k_.asel(w_[:], w_[:], 512, ALU.is_gt, -128 * r, -1, [Bw], [Bw])
                pend = (acc[:], v_sb[:, n, :], w_[:], first, last, [vB[(n * 128) // CH], Bw], [Bacc])
                if not last:
                    lc_n = lcr.next()
                    if first:
                        k_.cp(lc_n[0][:], l_[:], [Bl], [lc_n[1]])
                    else:
                        k_.tt(lc_n[0][:], lc_cur[0][:], l_[:], ALU.add, [lc_cur[1], Bl], [lc_n[1]])
                    lc_cur = lc_n
            yield
        k_.mm(*pend)
        o_, Bo = osr.next()
        k_.cp(o_[:], acc[:], [Bacc], [Bo])
        if fused_out is None:
            P.out_toks.append(k_.dma("sp", oT[:, qt * 512:(qt + 1) * 512], o_[:], osr.sem, [Bo], ()))
        else:
            SA, SH = fused_out
            dest, tl = qt // 8, qt % 8
            k_.dma("sp", SA[dest, :, tl * 512:(tl + 1) * 512], o_[:], osr.sem, [Bo], ())
            if tl == 7 and dest < 3:
                k_.dma("sp", SH[dest + 1, :, :], o_[:, 384:512], osr.sem, [Bo], ())

    pending = list(range(nqt - 1, -1, -1))
    active = {}
    nstep = 0
    while pending or active:
        nstep += 1
        if side is not None and nstep % side_every == 0:
            next(side, None)
        for sid in range(NSTR):
            if sid not in active and pending:
                active[sid] = tile_gen(pending.pop(0), sid)
        for sid in list(active.keys()):
            try:
                next(active[sid])
            except StopIteration:
                del active[sid]
    if side is not None:
        for _ in side:
            pass


def _emit_attn_pair(P, k_, env, seq, lag, fused_out, oT, side, side_every):
    q_sb, k_sb, v_sb = env["q_sb"], env["k_sb"], env["v_sb"]
    qB, kB, vB, CH = env["qB"], env["kB"], env["vB"], env["CH"]
    triI, Btri, nones, Bno = env["triI"], env["Btri"], env["nones"], env["Bno"]
    nqt = seq // 512
    zr = Rot(P, "a_z2", 3, [128, 1024], F32, psum=True)
    NSTR = env["nstream"]
    accs = [Rot(P, "a_acc%d" % i, 2 // NSTR, [128, 512], F32, psum=True) for i in range(NSTR)]
    er = Rot(P, "a_e2", 2 * NSTR, [128, 1024], F32)
    lr = Rot(P, "a_l2", NSTR * (lag + 1) + 1, [128, 1024], BF16)
    wr = Rot(P, "a_w2", 2 * NSTR + 1, [128, 1024], BF16)
    lcrs = [Rot(P, "a_lc%d" % i, 4, [128, 512], BF16) for i in range(NSTR)]
    osr = Rot(P, "a_os", 2, [128, 512], BF16)
    H = (slice(0, 512), slice(512, 1024))

    def tile_gen(qt, sid):
        acc, Bacc = accs[sid].next()
        lcr = lcrs[sid]
        blocks = list(range(4 * qt + 3, -1, -1))
        pairs = [(blocks[2 * i], blocks[2 * i + 1]) for i in range(len(blocks) // 2)]
        qs = slice(qt * 512, (qt + 1) * 512)
        st = {}
        lc_cur = None
        pend = []
        for i in range(len(pairs) + lag):
            if i < len(pairs):
                z, Bz = zr.next()
                e_, Be = er.next()
                l_, Bl = lr.next()
                for hh, n in enumerate(pairs[i]):
                    k_.mm(z[:, H[hh]], k_sb[:, n * 128:(n + 1) * 128], q_sb[:, qs], True, False,
                          [kB[(n * 128) // CH], qB[(qt * 512) // CH]], [Bz])
                k_.act(e_[:], z[:], AF.Exp, [Bz], [Be])
                k_.act(l_[:], e_[:], AF.Ln, [Be], [Bl], bias=1.0)
                for hh, n in enumerate(pairs[i]):
                    r = n - 4 * qt
                    if r >= 0:
                        k_.asel(l_[:, H[hh]], l_[:, H[hh]], 512, ALU.is_gt, -128 * r, -1, [Bl], [Bl])
                st[i] = (z, Bz, l_, Bl, pairs[i])
            j = i - lag
            if j >= 0:
                z, Bz, l_, Bl, (n0, n1) = st.pop(j)
                first = (j == 0)
                last = (n1 == 0)
                k_.mm(z[:, H[0]], triI[:], l_[:, H[0]], False, first, [Btri, Bl], [Bz])
                if not first:
                    k_.mm(z[:, H[0]], nones[:], lc_cur[0][:], False, True, [Bno, lc_cur[1]], [Bz])
                lc_mid = lcr.next()
                if first:
                    k_.cp(lc_mid[0][:], l_[:, H[0]], [Bl], [lc_mid[1]])
                else:
                    k_.tt(lc_mid[0][:], lc_cur[0][:], l_[:, H[0]], ALU.add, [lc_cur[1], Bl], [lc_mid[1]])
                for p in pend:
                    k_.mm(*p)
                pend = []
                k_.mm(z[:, H[1]], triI[:], l_[:, H[1]], False, False, [Btri, Bl], [Bz])
                k_.mm(z[:, H[1]], nones[:], lc_mid[0][:], False, True, [Bno, lc_mid[1]], [Bz])
                if not last:
                    lc_new = lcr.next()
                    k_.tt(lc_new[0][:], lc_mid[0][:], l_[:, H[1]], ALU.add, [lc_mid[1], Bl], [lc_new[1]])
                    lc_cur = lc_new
                w_, Bw = wr.next()
                k_.act(w_[:], z[:], AF.Exp, [Bz], [Bw])
                for hh, n in enumerate((n0, n1)):
                    r = n - 4 * qt
                    if r >= 0:
                        k_.asel(w_[:, H[hh]], w_[:, H[hh]], 512, ALU.is_gt, -128 * r, -1, [Bw], [Bw])
                pend = [(acc[:], v_sb[:, n0, :], w_[:, H[0]], first, False, [vB[(n0 * 128) // CH], Bw], [Bacc]),
                        (acc[:], v_sb[:, n1, :], w_[:, H[1]], False, last, [vB[(n1 * 128) // CH], Bw], [Bacc])]
            yield
        for p in pend:
            k_.mm(*p)
        o_, Bo = osr.next()
        k_.cp(o_[:], acc[:], [Bacc], [Bo])
        if fused_out is None:
            P.out_toks.append(k_.dma("sp", oT[:, qt * 512:(qt + 1) * 512], o_[:], osr.sem, [Bo], ()))
        else:
            SA, SH = fused_out
            dest, tl = qt // 8, qt % 8
            k_.dma("sp", SA[dest, :, tl * 512:(tl + 1) * 512], o_[:], osr.sem, [Bo], ())
            if tl == 7 and dest < 3:
                k_.dma("sp", SH[dest + 1, :, :], o_[:, 384:512], osr.sem, [Bo], ())

    pending = list(range(nqt - 1, -1, -1))
    active = {}
    nstep = 0
    while pending or active:
        nstep += 1
        if side is not None and nstep % side_every == 0:
            next(side, None)
        for sid in range(NSTR):
            if sid not in active and pending:
                active[sid] = tile_gen(pending.pop(0), sid)
        for sid in list(active.keys()):
            try:
                next(active[sid])
            except StopIteration:
                del active[sid]
    if side is not None:
        for _ in side:
            pass


def build_attn(seq=S):
    nc = bass.Bass("TRN2", target_bir_lowering=False)
    qT = nc.dram_tensor("qT", [128, seq], BF16, kind="ExternalInput").ap()
    kT = nc.dram_tensor("kT", [128, seq], BF16, kind="ExternalInput").ap()
    v = nc.dram_tensor("v", [seq, 128], BF16, kind="ExternalInput").ap()
    oT = nc.dram_tensor("oT", [128, seq], BF16, kind="ExternalOutput").ap()
    with ExitStack() as st:
        P = Prog(nc, st)
        emit_attn(nc, P, qT, kT, v, oT, seq=seq)
        P.finish()
        P.emit()
    return nc


class WRing:
    def __init__(self, P, name, ns, shape, units):
        self.P = P
        self.ns = ns
        self.t = [P.sb("%s%d" % (name, i), shape, BF16) for i in range(ns)]
        self.b = [Buf() for _ in range(ns)]
        self.sem = [P.dsem("%s%d" % (name, i)) for i in range(ns)]
        self.units = units
        self.nl = 0
        self.na = 0
        self.rel = set()
        self.pump()

    def pump(self):
        while self.nl < len(self.units) and (self.nl < self.ns or (self.nl - self.ns) in self.rel):
            s = self.nl % self.ns
            first = True
            tok = None
            for o, i in self.units[self.nl](self.t[s]):
                tok = self.P.op("pool", lambda e, o=o, i=i: e.dma_start(out=o, in_=i), (),
                                [self.b[s]] if first else (), dma=self.sem[s])
                first = False
            self.b[s].lw = tok
            self.nl += 1

    def acquire(self):
        k = self.na
        self.na += 1
        assert k < self.nl, "weight ring deadlock"
        return k, self.t[k % self.ns], self.b[k % self.ns]

    def release(self, k):
        self.rel.add(k)
        self.pump()


def wunit(w, r0, nk, c0, ncol=512):
    def f(t):
        src = w[r0:r0 + nk * 128, c0:c0 + ncol].rearrange("(kc p) n -> p kc n", p=128)
        return [(t[:, k0:k0 + 4, 0:ncol], src[:, k0:k0 + 4, :]) for k0 in range(0, nk, 4)]
    return f


def tile_units(d, pre, qkv):
    big, small = [], []
    if pre:
        wi, wg, wo, wu, wd = d["w_in"], d["w_gate"], d["w_o"], d["w_up"], d["w_dn"]
        for c0 in (2048, 1536, 2560, 3072):
            big.append(wunit(wi, 0, 16, c0))
        for cg in range(4):
            for i in range(3):
                big.append(wunit(wg, 0, 16, i * 2048 + cg * 512))
                small.append(wunit(d[("sbo", "cvo", "sgo")[i]], 0, 4, cg * 512))
        for cg in range(4):
            big.append(wunit(wo, 0, 16, cg * 512))
        for kh in range(2):
            for u in range(4):
                big.append(wunit(wu, 0, 16, (kh * 4 + u) * 512))
                big.append(wunit(wu, 0, 16, 4096 + (kh * 4 + u) * 512))
            for cg in range(4):
                big.append(wunit(wd, kh * 2048, 16, cg * 512))
    if qkv:
        for c0 in (0, 512, 1024):
            big.append(wunit(d["w_in_n"], 0, 16, c0))
    return big, small


def convert_gen(P, big, small, Wb, Wsb):
    k_ = K(P)
    rb_ = Rot(P, "cvb", 2, [128, 16, 512], BF16)
    rs_ = Rot(P, "cvs", 2, [128, 4, 512], BF16)
    for fns, rot, dst in ((big, rb_, Wb), (small, rs_, Wsb)):
        for u, fn in enumerate(fns):
            t, Bt = rot.next()
            first = True
            tok = None
            for o, i in fn(t):
                tok = P.op("pool", lambda e, o=o, i=i: e.dma_start(out=o, in_=i), (), [Bt] if first else (),
                           dma=rot.sem)
                first = False
            Bt.lw = tok
            k_.dma("sp", dst[u], t[:], P.dsem(rot.name + "_w%d" % rot.cur), [Bt], ())
            yield


PP_G1, PP_G2, PP_BG, PP_CVW, PP_CVB, PP_CVG, PP_CVBT, PP_FFW, PP_FFB = 0, 16, 32, 80, 204, 208, 212, 216, 312
NPP = 344


def emit_P(nc, P, d, pre, qkv, fin, ntile=8, halo=True, stop=None):
    k_ = K(P)
    xin = d["xin"]
    tiles = []
    off = 0
    if halo:
        tiles.append((0, 1, True))
        off = 128
    for i in range(ntile):
        tiles.append((off + i * 512, 4, False))

    dc = P.dsem("c")
    ident = P.sb("ident", [128, 128], BF16)
    Bid = Buf()
    onesf = P.sb("onesf", [128, 128], F32)
    Bof = Buf()
    onesb = P.sb("onesb", [128, 128], BF16)
    Bob = Buf()
    k_.memset(onesf[:], 1.0, [Bof], eng="dve")
    k_.cp(onesb[:], onesf[:], [Bof], [Bob])
    P.op("pool", lambda e: e.affine_select(out=ident[:], in_=onesf[:], pattern=[[1, 128]],
                                           compare_op=ALU.is_equal, fill=0.0, base=0, channel_multiplier=-1),
         [Bof], [Bid])
    Bpar = Buf()
    hm = P.sb("hmask", [128, 1], F32)
    k_.dma("sp", hm[:], d["hmask"][:, :], dc, (), [Bpar])

    def norm_params(pp_t, modt, gcol, shcol, sccol, name):
        a = P.sb(name + "_a", [128, 16], F32)
        k_.ts(a[:], modt[:, sccol:sccol + 16], 1.0, None, ALU.add, None, [Bpar], [Bpar])
        k_.tt(a[:], a[:], pp_t[:, gcol:gcol + 16], ALU.mult, [Bpar], [Bpar])
        return a

    if pre:
        pp = P.sb("pp", [128, NPP], F32)
        k_.dma("sp", pp[:], d["pp"][:, :], dc, (), [Bpar])
        modt = P.sb("modt", [128, 96], F32)
        k_.dma("sp", modt[:], d["modpp"][:, :], dc, (), [Bpar])
        rb = P.sb("rb", [128, 3, 512], F32)
        k_.dma("sp", rb[:], d["rb"][:, :, :], dc, (), [Bpar])
        wsT = P.sb("wsT", [128, 8, 128], BF16)
        k_.dma("pool", wsT[:], d["wsT"][:, :, :], P.dsem("cws"), (), [Bpar])
        k_.memset(wsT[64:128, :, 0:64], 0.0, [Bpar], eng="dve")
        a1 = norm_params(pp, modt, PP_G1, 0, 16, "n1")
        a2 = norm_params(pp, modt, PP_G2, 48, 64, "n2")
        b1 = modt[:, 0:16]
        b2 = modt[:, 48:64]
    if qkv:
        ppn = P.sb("ppn", [128, 16], F32)
        k_.dma("sp", ppn[:], d["g1n"][:, :], dc, (), [Bpar])
        modn = P.sb("modn", [128, 32], F32)
        k_.dma("sp", modn[:], d["modn"][:, :], dc, (), [Bpar])
        a1n = norm_params(ppn, modn, 0, 0, 16, "nn")
        b1n = modn[:, 0:16]

    if stop == 'params':
        return
    big_units = []
    small_units = []
    for (_, _, ish) in tiles:
        if "Wb" in d:
            nb_pre = 44 if pre else 0
            for u in range(nb_pre):
                if ish and u >= 20 and ((u - 20) % 12 >= 8 or (u - 20) % 2 == 1):
                    continue
                big_units.append(lambda t, u=u: [(t[:, 0:8, :], d["Wb"][u, :, 0:8, :]), (t[:, 8:16, :], d["Wb"][u, :, 8:16, :])])
            if pre:
                for u in range(12):
                    small_units.append(lambda t, u=u: [(t[:, :, :], d["Wsb"][u, :, :, :])])
            if qkv and not ish:
                for u in range(3):
                    big_units.append(lambda t, u=u: [(t[:, 0:8, :], d["Wb"][nb_pre + u, :, 0:8, :]),
                                                     (t[:, 8:16, :], d["Wb"][nb_pre + u, :, 8:16, :])])
        else:
            bu, su = tile_units(d, pre, qkv and not ish)
            if ish:
                bu = [f for u, f in enumerate(bu) if not (u >= 20 and ((u - 20) % 12 >= 8 or (u - 20) % 2 == 1))]
            big_units += bu
            small_units += su
    resident = (not pre) and qkv and "Wb" not in d
    if resident:
        big_units = big_units[0:3]

        class _Res:
            def __init__(self, ring):
                self.ring = ring
                self.k = 0

            def acquire(self):
                i = self.k % 3
                self.k += 1
                return i, self.ring.t[i], self.ring.b[i]

            def release(self, k):
                pass
    WB = WRing(P, "wb", 3, [128, 16, 512], big_units)
    if resident:
        WB = _Res(WB)
    if pre:
        WS = WRing(P, "ws", 2, [128, 4, 512], small_units)

    x_sb = P.sb("x_sb", [128, 4, 2048], F32)
    Bx = [Buf() for _ in range(4)]
    xsr = Rot(P, "xs", 1, [128, 2048], BF16)
    h_sb = P.sb("h_sb", [128, 16, 512], BF16)
    Bh = [Buf() for _ in range(16)]
    npst = 2
    pstr = Rot(P, "pst", npst, [128, 512], BF16, psum=True)
    bank = Rot(P, "bk", 8 - npst, [128, 512], F32, psum=True)
    tmpr = Rot(P, "tmp", 3, [128, 512], F32)
    smallr = Rot(P, "sml", 4, [128, 16], F32)
    dx = P.dsem("x")
    big = P.sb("big", [128, 16, 512], BF16)
    Bbig = [Buf() for _ in range(16)]
    xs4 = big[:].rearrange("p a b -> p (a b)")
    if pre:
        f32b = P.sb("f32b", [128, 4, 512], F32)
        Bf32 = [Buf() for _ in range(4)]
        glu = P.sb("glu", [128, 4, 30 + 512], BF16)
        cdgr = Rot(P, "cdiag", 2, [128, 31, 128], BF16)
        Bglu = [Buf() for _ in range(4)]
        k_.memset(glu[:], 0.0, Bglu, eng="dve")
        fh = P.sb("ffn_halo", [128, 32, 2], F32)
        Bfh = [Buf() for _ in range(32)]
        k_.memset(fh[:], 0.0, Bfh, eng="dve")
        ugr = Rot(P, "ug", 2, [128, 512], F32)
        vgr = Rot(P, "vg", 2, [128, 512], F32)
        conf = P.sb("conf", [128, 4, 512], BF16)
        Bconf = [Buf() for _ in range(4)]
        sgT = P.sb("sgT", [128, 4, 512], BF16)
        BsgT = Buf()
        at_sb = P.sb("at_sb", [128, 4, 512], BF16)
        Bat = Buf()
        b16r = Rot(P, "b16", 4, [128, 512], BF16)
        gpr = Rot(P, "gp", 2, [128, 2 + 512], F32)
        gtr = Rot(P, "gt", 2, [128, 512], F32)
        str_ = Rot(P, "st6", 2, [128, 6], F32)
    if qkv:
        stg = Rot(P, "stg", 3, [128, 512], BF16)

    def rstd_of(ss, n, inv_n, reads):
        r, Br = smallr.next()
        k_.ts(r[:, 0:n], ss, inv_n, EPS, ALU.mult, ALU.add, reads, [Br])
        k_.act(r[:, 0:n], r[:, 0:n], AF.Ln, [Br], [Br])
        k_.act(r[:, 0:n], r[:, 0:n], AF.Exp, [Br], [Br], scale=-0.5)
        return r, Br

    def norm_to_h(nb, a, b):
        T = nb * 128
        ss, Bss = smallr.next()
        for j in range(nb):
            k_.act(xs4[:, j * 2048:(j + 1) * 2048], x_sb[:, j, :], AF.Square, [Bx[j]],
                   Bbig[4 * j:4 * j + 4] + [Bss], accum=ss[:, j:j + 1])
        r, Br = rstd_of(ss[:, 0:nb], nb, 1.0 / D, [Bss])
        for j in range(nb):
            if j % 2 == 0:
                k_.act(xs4[:, j * 2048:(j + 1) * 2048], x_sb[:, j, :], AF.Identity, [Bx[j], Br],
                       Bbig[4 * j:4 * j + 4], scale=r[:, j:j + 1])
            else:
                k_.ts(xs4[:, j * 2048:(j + 1) * 2048], x_sb[:, j, :], r[:, j:j + 1], None, ALU.mult, None,
                      [Bx[j], Br], Bbig[4 * j:4 * j + 4])
        for c in range(16):
            pt, Bpt = pstr.next()
            for j in range(nb):
                k_.tr(pt[:, j * 128:(j + 1) * 128], xs4[:, j * 2048 + c * 128:j * 2048 + (c + 1) * 128], ident[:],
                      [Bbig[4 * j + c // 4], Bid], [Bpt])
            if c % 2 == 0:
                k_.act(h_sb[:, c, 0:T], pt[:, 0:T], AF.Identity, [Bpt, Bpar], [Bh[c]],
                       bias=b[:, c:c + 1], scale=a[:, c:c + 1])
            else:
                k_.ts(h_sb[:, c, 0:T], pt[:, 0:T], a[:, c:c + 1], b[:, c:c + 1], ALU.mult, ALU.add,
                      [Bpt, Bpar], [Bh[c]])

    def fm_group(wt, Bw, oc, nk, rhs_fn, rhs_bufs, T):
        bk, Bb = bank.next()
        for kc in range(nk):
            k_.mm(bk[:, 0:T], wt[:, kc, oc * 128:(oc + 1) * 128], rhs_fn(kc), kc == 0, kc == nk - 1,
                  [Bw] + rhs_bufs(kc), [Bb])
        return bk, Bb

    def tm_group(wt, Bw, j, nk, lhs_fn, lhs_bufs, bkB=None, first=True, lastg=True, kbase=0):
        if bkB is None:
            bkB = bank.next()
        bk, Bb = bkB
        for kc in range(nk):
            k_.mm(bk[:], lhs_fn(kbase + kc)[:, j * 128:(j + 1) * 128], wt[:, kc, :],
                  first and kc == 0, lastg and kc == nk - 1, [Bw] + lhs_bufs(kbase + kc), [Bb])
        return bk, Bb

    def resid_add(bk, Bb, j, cg, gtt, Bgt):
        t, Bt = tmpr.next()
        k_.tt(t[:], bk[:], gtt[:], ALU.mult, [Bb, Bgt], [Bt])
        xs_ = x_sb[:, j, cg * 512:(cg + 1) * 512]
        k_.tt(xs_, xs_, t[:], ALU.add, [Bt, Bx[j]], [Bx[j]])

    def gelu_tanh(dst, Bdst, bk, Bb, eng="dve"):
        k_.act(dst[:], bk[:], AF.Copy, [Bb], [Bdst])
        t, Bt = tmpr.next()
        k_.tt(t[:], dst[:], dst[:], ALU.mult, [Bdst], [Bt], eng=eng)
        k_.ts(t[:], t[:], 0.044715, 1.0, ALU.mult, ALU.add, [Bt], [Bt], eng=eng)
        k_.tt(t[:], t[:], dst[:], ALU.mult, [Bt, Bdst], [Bt], eng=eng)
        k_.act(t[:], t[:], AF.Sigmoid, [Bt], [Bt], scale=1.5957691216057308)
        k_.tt(dst[:], dst[:], t[:], ALU.mult, [Bt, Bdst], [Bdst], eng=eng)

    hfn = lambda kc: h_sb[:, kc, :]
    hbuf = lambda kc: [Bh[kc]]

    for (tok0, nb, ish) in tiles:
        T = nb * 128
        for j in range(nb):
            k_.dma("sp", x_sb[:, j, :], xin[tok0 + j * 128:tok0 + (j + 1) * 128, :], P.dsem("x%d" % j), (), [Bx[j]])
        if pre:
            k_.dma("sp", at_sb[:, :, 0:T], d["attnT"][:, :, tok0:tok0 + T].rearrange("h p t -> p h t"),
                   P.dsem("at"), (), [Bat])
            if stop == 'xload':
                return
            norm_to_h(nb, a1, b1)
            if stop == 'norm1':
                return
            def build_diag(i):
                cdg, Bcdg = cdgr.next()
                for j in range(31):
                    k_.ts(cdg[:, j, :], ident[:], pp[:, PP_CVW + i * 31 + j:PP_CVW + i * 31 + j + 1], None,
                          ALU.mult, None, [Bid, Bpar], [Bcdg])
                return cdg, Bcdg
            diags = [build_diag(0), build_diag(1)]
            ku, wt, Bw = WB.acquire()
            for i in range(4):
                bk, Bb = fm_group(wt, Bw, i, 16, lambda kc: h_sb[:, kc, 0:T], hbuf, T)
                k_.act(glu[:, i, 30:30 + T], bk[:, 0:T], AF.Sigmoid, [Bb], [Bglu[i]])
            WB.release(ku)
            ku, wt, Bw = WB.acquire()
            for i in range(4):
                bk, Bb = fm_group(wt, Bw, i, 16, lambda kc: h_sb[:, kc, 0:T], hbuf, T)
                k_.tt(glu[:, i, 30:30 + T], glu[:, i, 30:30 + T], bk[:, 0:T], ALU.mult, [Bb, Bglu[i]], [Bglu[i]])
            WB.release(ku)
            if stop == 'glu':
                return
            s1, Bs1 = bank.next()
            s2, Bs2 = bank.next()
            pend_stats = []
            for i in range(4):
                y = f32b[:, i, 0:T]
                cdg, Bcdg = diags[i]
                cb, Bcb = bank.next()
                for j in range(31):
                    k_.mm(cb[:, 0:T], cdg[:, j, :], glu[:, i, j:j + T], j == 0, j == 30, [Bcdg, Bglu[i]], [Bcb])
                for pa in pend_stats:
                    k_.mm(*pa)
                pend_stats = []
                if i + 2 < 4:
                    diags.append(build_diag(i + 2))
                k_.act(y, cb[:, 0:T], AF.Identity, [Bcb, Bpar], [Bf32[i]], bias=pp[:, PP_CVB + i:PP_CVB + i + 1])
                if ish:
                    k_.ts(glu[:, i, 0:30], glu[:, i, T:T + 30], hm[:, 0:1], None, ALU.mult, None,
                          [Bglu[i], Bpar], [Bglu[i]])
                else:
                    k_.cp(glu[:, i, 0:30], glu[:, i, T:T + 30], [Bglu[i]], [Bglu[i]])
                yb, Byb = b16r.next()
                k_.cp(yb[:, 0:T], y, [Bf32[i]], [Byb])
                ysq, Bysq = b16r.next()
                k_.act(ysq[:, 0:T], y, AF.Square, [Bf32[i]], [Bysq])
                pend_stats = [(s1[:, 0:T], onesb[:], yb[:, 0:T], i == 0, i == 3, [Bob, Byb], [Bs1]),
                              (s2[:, 0:T], onesb[:], ysq[:, 0:T], i == 0, i == 3, [Bob, Bysq], [Bs2])]
            for pa in pend_stats:
                k_.mm(*pa)
            mean, Bmean = tmpr.next()
            k_.ts(mean[:, 0:T], s1[:, 0:T], 1.0 / 512, None, ALU.mult, None, [Bs1], [Bmean])
            var, Bvar = tmpr.next()
            k_.tt(var[:, 0:T], mean[:, 0:T], mean[:, 0:T], ALU.mult, [Bmean], [Bvar])
            k_.stt(var[:, 0:T], s2[:, 0:T], 1.0 / 512, var[:, 0:T], ALU.mult, ALU.subtract, [Bs2, Bvar], [Bvar])
            k_.ts(var[:, 0:T], var[:, 0:T], EPS, None, ALU.add, None, [Bvar], [Bvar])
            k_.act(var[:, 0:T], var[:, 0:T], AF.Ln, [Bvar], [Bvar])
            k_.act(var[:, 0:T], var[:, 0:T], AF.Exp, [Bvar], [Bvar], scale=-0.5)
            for i in range(4):
                y = f32b[:, i, 0:T]
                k_.tt(y, y, mean[:, 0:T], ALU.subtract, [Bf32[i], Bmean], [Bf32[i]])
                k_.tt(y, y, var[:, 0:T], ALU.mult, [Bf32[i], Bvar], [Bf32[i]])
                k_.act(conf[:, i, 0:T], y, AF.Silu, [Bf32[i], Bpar], [Bconf[i]],
                       bias=pp[:, PP_CVBT + i:PP_CVBT + i + 1], scale=pp[:, PP_CVG + i:PP_CVG + i + 1])
            if stop == 'conf':
                return
            kuu, wtu, Bwu = WB.acquire()
            kuv, wtv, Bwv = WB.acquire()
            def sg_a(j):
                bu, Bbu = tm_group(wtu, Bwu, j, 16, hfn, hbuf)
                ug, Bug = ugr.next()
                gelu_tanh(ug, Bug, bu, Bbu)
                bv, Bbv = tm_group(wtv, Bwv, j, 16, hfn, hbuf)
                vg, Bvg = vgr.next()
                gelu_tanh(vg, Bvg, bv, Bbv, eng="pool")
                return ug, Bug, vg, Bvg

            def sg_b(j, ug, Bug, vg, Bvg):
                st6, Bst = str_.next()
                P.op("dve", lambda e, st6=st6, vg=vg: e.bn_stats(out=st6[:], in_=vg[:]), [Bvg], [Bst])
                mv, Bmv = smallr.next()
                P.op("dve", lambda e, st6=st6, mv=mv: e.bn_aggr(out=mv[:, 0:2], in_=st6[:]), [Bst], [Bmv])
                k_.ts(mv[:, 2:3], mv[:, 1:2], EPS, None, ALU.add, None, [Bmv], [Bmv])
                k_.act(mv[:, 2:3], mv[:, 2:3], AF.Ln, [Bmv], [Bmv])
                k_.act(mv[:, 2:3], mv[:, 2:3], AF.Exp, [Bmv], [Bmv], scale=-0.5)
                k_.ts(vg[:], vg[:], mv[:, 0:1], mv[:, 2:3], ALU.subtract, ALU.mult, [Bvg, Bmv], [Bvg])
                k_.tt(vg[:], vg[:], rb[:, 0, :], ALU.mult, [Bvg, Bpar], [Bvg])
                vn, Bvn = b16r.next()
                k_.tt(vn[:], vg[:], rb[:, 1, :], ALU.add, [Bvg, Bpar], [Bvn])
                bm, Bbm = bank.next()
                for g in range(8):
                    k_.mm(bm[:, g * 64:(g + 1) * 64], wsT[:, g, :], vn[:, g * 64:(g + 1) * 64], True, True,
                          [Bpar, Bvn], [Bbm])
                t1, Bt1 = tmpr.next()
                k_.tt(t1[:], bm[:], rb[:, 2, :], ALU.add, [Bbm, Bpar], [Bt1])
                ysg, Bysg = b16r.next()
                k_.tt(ysg[:], t1[:], ug[:], ALU.mult, [Bt1, Bug], [Bysg])
                pt, Bpt = pstr.next()
                for cc in range(4):
                    k_.tr(pt[:, cc * 128:(cc + 1) * 128], ysg[:, cc * 128:(cc + 1) * 128], ident[:],
                          [Bysg, Bid], [Bpt])
                for cc in range(4):
                    k_.cp(sgT[:, cc, j * 128:(j + 1) * 128], pt[:, cc * 128:(cc + 1) * 128], [Bpt], [BsgT])

            nxt = sg_a(0)
            for j in range(nb):
                cur = nxt
                if j + 1 < nb:
                    nxt = sg_a(j + 1)
                sg_b(j, *cur)
            WB.release(kuu)
            WB.release(kuv)
            if stop == 'sg':
                return
            brs = [(at_sb, lambda kc: [Bat]), (conf, lambda kc: [Bconf[kc]]), (sgT, lambda kc: [BsgT])]
            for cg in range(4):
                for i in range(3):
                    kg, wg_, Bwg = WB.acquire()
                    ks, wsm, Bws = WS.acquire()
                    src, sbuf = brs[i]
                    for oc in range(4):
                        c = cg * 4 + oc
                        bg, Bbg = fm_group(wg_, Bwg, oc, 16, lambda kc: h_sb[:, kc, 0:T], hbuf, T)
                        g_, Bg = tmpr.next()
                        k_.act(g_[:, 0:T], bg[:, 0:T], AF.Sigmoid, [Bbg, Bpar], [Bg],
                               bias=pp[:, PP_BG + i * 16 + c:PP_BG + i * 16 + c + 1])
                        by, Bby = fm_group(wsm, Bws, oc, 4, lambda kc, src=src: src[:, kc, 0:T], sbuf, T)
                        m = f32b[:, oc, 0:T]
                        if i == 0:
                            k_.tt(m, g_[:, 0:T], by[:, 0:T], ALU.mult, [Bg, Bby], [Bf32[oc]])
                        else:
                            k_.tt(g_[:, 0:T], g_[:, 0:T], by[:, 0:T], ALU.mult, [Bg, Bby], [Bg])
                            if i == 1:
                                k_.tt(m, m, g_[:, 0:T], ALU.add, [Bg, Bf32[oc]], [Bf32[oc]])
                            else:
                                k_.tt(big[:, c, 0:T], m, g_[:, 0:T], ALU.add, [Bg, Bf32[oc]], [Bbig[c]])
                    WB.release(kg)
                    WS.release(ks)
            if stop == 'gates':
                return
            for cg in range(4):
                kw, wt, Bw = WB.acquire()
                gtt, Bgt = gtr.next()
                k_.dma("sp", gtt[:], d["gt1"][:, cg * 512:(cg + 1) * 512], gtr.sem, (), [Bgt])
                for j in range(nb):
                    bk, Bb = tm_group(wt, Bw, j, 16, lambda kc: big[:, kc, :], lambda kc: [Bbig[kc]])
                    resid_add(bk, Bb, j, cg, gtt, Bgt)
                WB.release(kw)
            if stop == 'wo':
                return
            norm_to_h(nb, a2, b2)
            for kh in range(2):
                for u in range(4):
                    kg, wg_, Bwg = WB.acquire()
                    if not ish:
                        kv, wv_, Bwv_ = WB.acquire()
                    for oc in range(4):
                        hc = (kh * 4 + u) * 4 + oc
                        ac = u * 4 + oc
                        bg, Bbg = fm_group(wg_, Bwg, oc, 16, lambda kc: h_sb[:, kc, 0:T], hbuf, T)
                        gp, Bgp = gpr.next()
                        k_.act(gp[:, 2:2 + T], bg[:, 0:T], AF.Copy, [Bbg], [Bgp])
                        k_.cp(gp[:, 0:2], fh[:, hc, :], [Bfh[hc]], [Bgp])
                        cw = PP_FFW + hc * 3
                        acc_, Bacc_ = tmpr.next()
                        k_.ts(acc_[:, 0:T], gp[:, 0:T], pp[:, cw:cw + 1], pp[:, PP_FFB + hc:PP_FFB + hc + 1],
                              ALU.mult, ALU.add, [Bgp, Bpar], [Bacc_])
                        k_.stt(acc_[:, 0:T], gp[:, 1:1 + T], pp[:, cw + 1:cw + 2], acc_[:, 0:T], ALU.mult, ALU.add,
                               [Bgp, Bpar, Bacc_], [Bacc_])
                        k_.stt(acc_[:, 0:T], gp[:, 2:2 + T], pp[:, cw + 2:cw + 3], acc_[:, 0:T], ALU.mult, ALU.add,
                               [Bgp, Bpar, Bacc_], [Bacc_])
                        if ish:
                            k_.ts(fh[:, hc, :], gp[:, T:T + 2], hm[:, 0:1], None, ALU.mult, None,
                                  [Bgp, Bpar], [Bfh[hc]])
                        else:
                            k_.cp(fh[:, hc, :], gp[:, T:T + 2], [Bgp], [Bfh[hc]])
                        if ish:
                            continue
                        k_.act(acc_[:, 0:T], acc_[:, 0:T], AF.Silu, [Bacc_], [Bacc_])
                        bv, Bbv = fm_group(wv_, Bwv_, oc, 16, lambda kc: h_sb[:, kc, 0:T], hbuf, T)
                        k_.tt(big[:, ac, 0:T], acc_[:, 0:T], bv[:, 0:T], ALU.mult, [Bacc_, Bbv], [Bbig[ac]])
                    WB.release(kg)
                    if not ish:
                        WB.release(kv)
                for cg in range(4):
                    if ish:
                        break
                    kw, wt, Bw = WB.acquire()
                    gtt, Bgt = gtr.next()
                    k_.dma("sp", gtt[:], d["gt2"][:, cg * 512:(cg + 1) * 512], gtr.sem, (), [Bgt])
                    for j in range(nb):
                        bk, Bb = tm_group(wt, Bw, j, 16, lambda kc: big[:, kc, :], lambda kc: [Bbig[kc]])
                        resid_add(bk, Bb, j, cg, gtt, Bgt)
                    WB.release(kw)
            if stop == 'ffn':
                return
            if not fin and not ish:
                for j in range(nb):
                    P.out_toks.append(
                        k_.dma("sp", d["xout"][tok0 + j * 128:tok0 + (j + 1) * 128, :], x_sb[:, j, :],
                               P.dsem("xo%d" % j), [Bx[j]], ()))
        if fin and not ish:
            ss, Bss = smallr.next()
            for j in range(nb):
                xs, Bxs = xsr.next()
                k_.act(xs[:], x_sb[:, j, :], AF.Square, [Bx[j]], [Bxs, Bss], accum=ss[:, j:j + 1])
            r, Br = rstd_of(ss[:, 0:nb], nb, 1.0 / D, [Bss])
            for j in range(nb):
                for q4 in range(4):
                    o_ = f32b[:, q4, :]
                    gtt, Bgt = gtr.next()
                    k_.dma("sp", gtt[:], d["gfin"][:, q4 * 512:(q4 + 1) * 512], gtr.sem, (), [Bgt])
                    k_.stt(o_, x_sb[:, j, q4 * 512:(q4 + 1) * 512], r[:, j:j + 1], gtt[:],
                           ALU.mult, ALU.mult, [Bx[j], Br, Bgt], [Bf32[q4]])
                otok = tok0 - (128 if halo else 0) + j * 128
                P.out_toks.append(
                    k_.dma("sp", d["out"][otok:otok + 128, :], f32b[:].rearrange("p a b -> p (a b)"), P.dsem("fo"),
                           Bf32, ()))
        if qkv and not ish:
            norm_to_h(nb, a1n, b1n)
            qtok = tok0 - (128 if halo else 0)
            for which in range(2):
                kw, wt, Bw = WB.acquire()
                dst = d["qT"] if which == 0 else d["kT"]
                for oc in range(4):
                    bk, Bb = fm_group(wt, Bw, oc, 16, lambda kc: h_sb[:, kc, 0:T], hbuf, T)
                    s_, Bs_ = stg.next()
                    if which == 0:
                        k_.act(s_[:, 0:T], bk[:, 0:T], AF.Copy, [Bb], [Bs_], scale=float(1.0 / np.sqrt(DH)))
                    else:
                        k_.cp(s_[:, 0:T], bk[:, 0:T], [Bb], [Bs_])
                    P.out_toks.append(k_.dma("sp", dst[oc, :, qtok:qtok + T], s_[:, 0:T], stg.sem, [Bs_], ()))
                WB.release(kw)
            kw, wt, Bw = WB.acquire()
            for j in range(nb):
                bk, Bb = tm_group(wt, Bw, j, 16, hfn, hbuf)
                s_, Bs_ = stg.next()
                k_.cp(s_[:], bk[:], [Bb], [Bs_])
                if "v_hpnd" in d:
                    nblk_ = qtok // 128 + j
                    P.out_toks.append(k_.dma("sp", d["v_hpnd"][:, :, nblk_, :].rearrange("h p d -> p h d"),
                                             s_[:].rearrange("p (h d) -> p h d", h=4), stg.sem, [Bs_], ()))
                else:
                    P.out_toks.append(k_.dma("sp", d["v"][qtok + j * 128:qtok + (j + 1) * 128, :], s_[:], stg.sem, [Bs_], ()))
            WB.release(kw)


def build_P(pre, qkv, fin, ntile=8, halo=True, stop=None):
    nc = bass.Bass("TRN2", target_bir_lowering=False)
    ntok = ntile * 512 + (128 if halo else 0)
    nown = ntile * 512
    d = {}

    def inp(name, shape, dt=F32):
        d[name] = nc.dram_tensor(name, list(shape), dt, kind="ExternalInput").ap()

    def outp(name, shape, dt=F32):
        d[name] = nc.dram_tensor(name, list(shape), dt, kind="ExternalOutput").ap()

    inp("xin", [ntok, D])
    inp("hmask", [128, 1])
    if pre:
        inp("pp", [128, NPP])
        inp("modpp", [128, 96])
        inp("rb", [128, 3, 512])
        inp("wsT", [128, 8, 128])
        inp("gt1", [128, D])
        inp("gt2", [128, D])
        inp("attnT", [4, 128, ntok], BF16)
        inp("w_in", [D, 3584])
        inp("w_gate", [D, 3 * D])
        inp("sbo", [512, D])
        inp("cvo", [512, D])
        inp("sgo", [512, D])
        inp("w_o", [D, D])
        inp("w_up", [D, 4 * D])
        inp("w_dn", [2 * D, D])
        if not fin:
            outp("xout", [ntok, D])
    if qkv:
        inp("g1n", [128, 16])
        inp("modn", [128, 32])
        inp("w_in_n", [D, 3584])
        outp("qT", [4, 128, nown], BF16)
        outp("kT", [4, 128, nown], BF16)
        outp("v", [nown, 512], BF16)
    if fin:
        inp("gfin", [128, D])
        outp("out", [nown, D])
    with ExitStack() as st:
        P = Prog(nc, st)
        emit_P(nc, P, d, pre, qkv, fin, ntile=ntile, halo=halo, stop=stop)
        P.finish()
        P.emit()
    return nc


def emit_mod(nc, P, d, ncc=96, collective=None):
    k_ = K(P)
    nun = ncc // 4
    Bpar = Buf()
    dc = P.dsem("mc")
    cpp = P.sb("m_cpp", [128, 16], F32)
    bapp = P.sb("m_bapp", [128, L_, ncc], F32)
    k_.dma("sp", cpp[:], d["cpp"][:, :], dc, (), [Bpar])
    k_.dma("sp", bapp[:], d["bapp"][:, :, :], dc, (), [Bpar])
    cond = P.sb("m_cond", [128, 16, 2], BF16)
    Bc = Buf()
    for r in range(2):
        k_.act(cond[:, :, r], cpp[:], AF.Silu, [Bpar], [Bc])
    units = []
    for l in range(L_):
        for u in range(nun):
            units.append(wunit(d["w_ada"][l], 0, 16, u * 512))
    WB = WRing(P, "mw", 3, [128, 16, 512], units)
    pm = Rot(P, "m_ps", 2, [128, ncc, 2], F32, psum=True)
    res = P.sb("m_res", [128, L_, 96], F32)
    resp = P.sb("m_resp", [128, L_, ncc], F32)
    Bres = Buf()
    for l in range(L_):
        ps, Bps = pm.next()
        for u in range(nun):
            ku, wt, Bw = WB.acquire()
            for oc in range(4):
                j = u * 4 + oc
                for kc in range(16):
                    k_.mm(ps[:, j, :], wt[:, kc, oc * 128:(oc + 1) * 128], cond[:, kc, :], kc == 0, kc == 15,
                          [Bw, Bc], [Bps])
            WB.release(ku)
        k_.tt((res if collective is None else resp)[:, l, :], ps[:, :, 0], bapp[:, l, :], ALU.add,
              [Bps, Bpar], [Bres])
    if collective is not None:
        MP, MG = d["MP"], d["MG"]
        k_.dma("sp", MP[:, :], resp[:].rearrange("p l c -> p (l c)"), P.dsem("mp"), [Bres], ())
        P.barrier()
        collective(MP[:, :], MG[:, :])
        P.barrier()
        for s4 in range(4):
            k_.dma("sp", res[:, :, s4 * ncc:(s4 + 1) * ncc],
                   MG[s4 * 128:(s4 + 1) * 128, :].rearrange("p (l c) -> p l c", l=L_), P.dsem("mg"), (), [Bres])
    P.out_toks.append(k_.dma("sp", d["modpp"][:, :, :], res[:], P.dsem("mo"), [Bres], ()))
    if "gtb" in d:
        onesf = P.sb("m_onesf", [128, 128], F32)
        identf = P.sb("m_identf", [128, 128], F32)
        Bk = Buf()
        k_.memset(onesf[:], 1.0, [Bk])
        P.op("pool", lambda e: e.affine_select(out=identf[:], in_=onesf[:], pattern=[[1, 128]],
                                               compare_op=ALU.is_equal, fill=0.0, base=0, channel_multiplier=-1),
             [Bk], [Bk])
        lr = Rot(P, "m_lh", 2, [128, 128], F32)
        gr = Rot(P, "m_gt", 2, [128, D], F32)
        pg = Rot(P, "m_pg", 2, [128, 512], F32, psum=True)
        for l in range(L_):
            for kk, sec in enumerate((2, 5)):
                g_, Bg = gr.next()
                for c4 in range(4):
                    pb, Bpb = pg.next()
                    for cc in range(4):
                        c = c4 * 4 + cc
                        lh, Blh = lr.next()
                        k_.ts(lh[:], onesf[:], res[:, l, sec * 16 + c:sec * 16 + c + 1], None, ALU.mult, None,
                              [Bk, Bres], [Blh])
                        k_.mm(pb[:, cc * 128:(cc + 1) * 128], lh[:], identf[:], True, True, [Blh, Bk], [Bpb])
                    k_.cp(g_[:, c4 * 512:(c4 + 1) * 512], pb[:], [Bpb], [Bg])
                P.out_toks.append(k_.dma("sp", d["gtb"][l, kk, :, :], g_[:], gr.sem, [Bg], ()))


def build_mod():
    nc = bass.Bass("TRN2", target_bir_lowering=False)
    d = {}
    d["cpp"] = nc.dram_tensor("cpp", [128, 16], F32, kind="ExternalInput").ap()
    d["bapp"] = nc.dram_tensor("bapp", [128, L_, 96], F32, kind="ExternalInput").ap()
    d["w_ada"] = nc.dram_tensor("w_ada", [L_, D, 6 * D], F32, kind="ExternalInput").ap()
    d["modpp"] = nc.dram_tensor("modpp", [128, L_, 96], F32, kind="ExternalOutput").ap()
    with ExitStack() as st:
        P = Prog(nc, st)
        emit_mod(nc, P, d)
        P.finish()
        P.emit()
    return nc


def pp_of(vec, n):
    return np.ascontiguousarray(np.asarray(vec, np.float32).reshape(n, 128).T)


def layer_inputs(W, l, modpp_l):
    pp = np.zeros((128, NPP), np.float32)
    pp[:, PP_G1:PP_G1 + 16] = pp_of(W["g_norm1"][l], 16)
    pp[:, PP_G2:PP_G2 + 16] = pp_of(W["g_norm2"][l], 16)
    pp[:, PP_BG:PP_BG + 48] = pp_of(W["b_gate"][l], 48)
    pp[:, PP_CVW:PP_CVW + 124] = W["cv_w_dw"][l].T.reshape(4, 128, 31).transpose(1, 0, 2).reshape(128, 124)
    pp[:, PP_CVB:PP_CVB + 4] = pp_of(W["cv_b_dw"][l], 4)
    pp[:, PP_CVG:PP_CVG + 4] = pp_of(W["cv_ln_g"][l], 4)
    pp[:, PP_CVBT:PP_CVBT + 4] = pp_of(W["cv_ln_b"][l], 4)
    pp[:, PP_FFW:PP_FFW + 96] = W["ffn_w_dw"][l].T.reshape(32, 128, 3).transpose(1, 0, 2).reshape(128, 96)
    pp[:, PP_FFB:PP_FFB + 32] = pp_of(W["ffn_b_dw"][l], 32)
    rb = np.zeros((128, 3, 512), np.float32)
    rb[:, 0, :] = W["sg_ln_g"][l][None, :]
    rb[:, 1, :] = W["sg_ln_b"][l][None, :]
    rb[:, 2, :] = np.repeat(W["sg_b_s"][l].T, 64, axis=1)
    wsT = np.ascontiguousarray(W["sg_w_s"][l].transpose(2, 0, 1))
    gt1 = np.ascontiguousarray(np.broadcast_to(modpp_l[:, 32:48].T.reshape(1, D), (128, D)))
    gt2 = np.ascontiguousarray(np.broadcast_to(modpp_l[:, 80:96].T.reshape(1, D), (128, D)))
    return dict(pp=pp, modpp=np.ascontiguousarray(modpp_l), rb=rb, wsT=wsT, gt1=gt1, gt2=gt2,
                w_in=W["w_in"][l], w_gate=W["w_gate"][l], sbo=W["sb_w_out"][l], cvo=W["cv_w_out"][l],
                sgo=W["sg_w_out"][l], w_o=W["w_o"][l], w_up=W["ffn_w_up"][l], w_dn=W["ffn_w_down"][l])


def with_halo(a, t0, n, axis=0):
    if t0 >= 128:
        sl = [slice(None)] * a.ndim
        sl[axis] = slice(t0 - 128, t0 + n)
        return np.ascontiguousarray(a[tuple(sl)])
    shp = list(a.shape)
    shp[axis] = 128 + n
    out = np.zeros(shp, a.dtype)
    sl_o = [slice(None)] * a.ndim
    sl_o[axis] = slice(128, 128 + n)
    sl_i = [slice(None)] * a.ndim
    sl_i[axis] = slice(0, n)
    out[tuple(sl_o)] = a[tuple(sl_i)]
    return out


def run_attn(nc_attn, q_parts, k_parts, v_parts):
    in_maps = []
    for b in range(NB_):
        qT = np.concatenate([q_parts[b * 4 + s] for s in range(4)], axis=2)
        kT = np.concatenate([k_parts[b * 4 + s] for s in range(4)], axis=2)
        v = np.concatenate([v_parts[b * 4 + s] for s in range(4)], axis=0)
        for h in range(NH):
            in_maps.append({"qT": np.ascontiguousarray(qT[h]), "kT": np.ascontiguousarray(kT[h]),
                            "v": np.ascontiguousarray(v[:, h * 128:(h + 1) * 128])})
    res = run_bass_kernel_spmd(nc_attn, in_maps, core_ids=list(range(NCORE)))
    return [[np.asarray(res.results[b * 4 + h]["oT"]) for h in range(NH)] for b in range(NB_)]


def kernel_unfused(**inputs):
    W = {k: np.asarray(v) for k, v in inputs.items()}
    x = W["x"]
    c = W["c"]
    cores = [(b, s) for b in range(NB_) for s in range(4)]
    nc_m = build_mod()
    bapp = np.ascontiguousarray(np.stack([pp_of(W["b_ada"][l], 96) for l in range(L_)], axis=1))
    res = run_bass_kernel_spmd(nc_m, [{"cpp": pp_of(c[b], 16), "bapp": bapp, "w_ada": W["w_ada"]}
                                      for b in range(NB_)], core_ids=list(range(NB_)))
    modpp = [np.asarray(res.results[b]["modpp"]) for b in range(NB_)]
    ones_hm = np.ones((128, 1), np.float32)
    zeros_hm = np.zeros((128, 1), np.float32)
    nc_p0 = build_P(False, True, False, halo=False)
    in_maps = []
    for (b, s) in cores:
        in_maps.append({"xin": np.ascontiguousarray(x[b, s * TOKC:(s + 1) * TOKC]), "hmask": ones_hm,
                        "g1n": pp_of(W["g_norm1"][0], 16),
                        "modn": np.ascontiguousarray(modpp[b][:, 0, 0:32]), "w_in_n": W["w_in"][0]})
    res = run_bass_kernel_spmd(nc_p0, in_maps, core_ids=list(range(NCORE)))
    qp = [np.asarray(r["qT"]) for r in res.results]
    kp = [np.asarray(r["kT"]) for r in res.results]
    vp = [np.asarray(r["v"]) for r in res.results]
    nc_attn = build_attn()
    xcur = x
    out = None
    for l in range(L_):
        oT = run_attn(nc_attn, qp, kp, vp)
        last = (l == L_ - 1)
        nc_p = build_P(True, not last, last)
        in_maps = []
        for (b, s) in cores:
            t0 = s * TOKC
            m = layer_inputs(W, l, modpp[b][:, l, :])
            m["xin"] = with_halo(xcur[b], t0, TOKC, axis=0)
            m["hmask"] = zeros_hm if s == 0 else ones_hm
            m["attnT"] = np.stack([with_halo(oT[b][h], t0, TOKC, axis=1) for h in range(NH)], axis=0)
            if not last:
                m["g1n"] = pp_of(W["g_norm1"][l + 1], 16)
                m["modn"] = np.ascontiguousarray(modpp[b][:, l + 1, 0:32])
                m["w_in_n"] = W["w_in"][l + 1]
            else:
                m["gfin"] = np.ascontiguousarray(np.broadcast_to(W["g_final"][None, :], (128, D)))
            in_maps.append(m)
        res = run_bass_kernel_spmd(nc_p, in_maps, core_ids=list(range(NCORE)))
        if not last:
            qp = [np.asarray(r["qT"]) for r in res.results]
            kp = [np.asarray(r["kT"]) for r in res.results]
            vp = [np.asarray(r["v"]) for r in res.results]
            xn = np.empty_like(x)
            for i, (b, s) in enumerate(cores):
                xn[b, s * TOKC:(s + 1) * TOKC] = np.asarray(res.results[i]["xout"])[128:]
            xcur = xn
        else:
            out = np.empty_like(x)
            for i, (b, s) in enumerate(cores):
                out[b, s * TOKC:(s + 1) * TOKC] = np.asarray(res.results[i]["out"])
    return out


U32 = mybir.dt.uint32
LAYER_W = (("w_in", [D, 3584]), ("w_gate", [D, 3 * D]), ("sbo", [512, D]), ("cvo", [512, D]), ("sgo", [512, D]),
           ("w_o", [D, D]), ("w_up", [D, 4 * D]), ("w_dn", [2 * D, D]))
GROUPS = [[0, 1, 2, 3], [4, 5, 6, 7]]


def build_fused():
    nc = bass.Bass("TRN2", target_bir_lowering=False)
    NT = HALO + TOKC
    E = {}

    def inp(name, shape, dt=F32):
        E[name] = nc.dram_tensor(name, list(shape), dt, kind="ExternalInput").ap()

    def internal(name, shape, dt):
        return nc.dram_tensor(name, list(shape), dt, kind="Internal").ap()

    inp("xin", [NT, D])
    inp("hmask", [128, 1])
    inp("cpp", [128, 16])
    inp("bapp", [128, L_, 24])
    inp("idx", [128, 32], U32)
    inp("w_ada", [L_, D, 6 * D // 4])
    inp("gfin", [128, D])
    for l in range(L_):
        inp("pp%d" % l, [128, NPP])
        inp("rb%d" % l, [128, 3, 512])
        inp("wsT%d" % l, [128, 8, 128])
        for nm, shp in LAYER_W:
            inp("%s%d" % (nm, l), shp)
    out = nc.dram_tensor("out", [TOKC, D], F32, kind="ExternalOutput").ap()
    modpp = internal("modpp_i", [128, L_, 96], F32)
    gtb = internal("gtb_i", [L_, 2, 128, D], F32)
    SQ = [internal("SQ%d" % l, [4, 3, 128, 4096], BF16) for l in range(L_)]
    G = [internal("G%d" % l, [4 * 3 * 4 * 128, 4096], BF16) for l in range(L_)]
    SA = [internal("SA%d" % l, [4, 128, 4096], BF16) for l in range(L_)]
    SH = [internal("SH%d" % l, [4, 128, 128], BF16) for l in range(L_)]
    A = [internal("A%d" % l, [4 * 4 * 128, 4096], BF16) for l in range(L_)]
    AH = [internal("AH%d" % l, [4 * 4 * 128, 128], BF16) for l in range(L_)]
    ATL = [internal("ATL%d" % l, [4, 128, NT], BF16) for l in range(L_)]
    WBS = [internal("WBS%d" % l, [53, 128, 16, 512], BF16) for l in range(L_)]
    WSS = [internal("WSS%d" % l, [12, 128, 4, 512], BF16) for l in range(L_)]
    MP = internal("MP", [128, L_ * 24], F32)
    MG = internal("MG", [4 * 128, L_ * 24], F32)
    X1 = internal("X1", [NT, D], F32)
    XH = internal("XH", [4 * 128, D], F32)

    def layer_d(l):
        dd = {"pp": E["pp%d" % l], "modpp": modpp[:, l, :], "rb": E["rb%d" % l], "wsT": E["wsT%d" % l],
              "gt1": gtb[l, 0], "gt2": gtb[l, 1], "attnT": ATL[l], "hmask": E["hmask"]}
        for nm, _ in LAYER_W:
            dd[nm] = E["%s%d" % (nm, l)]
        return dd

    def qkv_d(l):
        return {"g1n": E["pp%d" % l][:, PP_G1:PP_G1 + 16], "modn": modpp[:, l, 0:32], "w_in_n": E["w_in%d" % l],
                "qT": SQ[l][:, 0], "kT": SQ[l][:, 1],
                "v_hpnd": SQ[l][:, 2].rearrange("h p (n d) -> h p n d", d=128), "hmask": E["hmask"]}

    with ExitStack() as gs:
        P = Prog(nc, gs)
        k_ = K(P)
        ccB = Buf()

        def collective(src, dst):
            P.op("pool", lambda e: e.collective_compute("AllGather", ALU.bypass, replica_groups=GROUPS,
                                                        ins=[src], outs=[dst]),
                 (), [ccB], dma=P.dsem("cc"), dinc=1)

        def phase(name):
            st = ExitStack()
            P.begin_phase(name + "_", st)
            return st

        with phase("m"):
            emit_mod(nc, P, {"cpp": E["cpp"], "bapp": E["bapp"], "w_ada": E["w_ada"], "modpp": modpp, "gtb": gtb,
                             "MP": MP, "MG": MG}, ncc=24, collective=collective)
            P.end_phase()
        with phase("p0"):
            d0 = qkv_d(0)
            d0["xin"] = E["xin"][HALO:NT, :]
            emit_P(nc, P, d0, False, True, False, halo=False)
            P.end_phase()
        for l in range(L_):
            last = (l == L_ - 1)
            with phase("xq%d" % l):
                for h in range(4):
                    for which in range(3):
                        r0 = ((h * 3 + which) * 4) * 128
                        collective(SQ[l][h, which], G[l][r0:r0 + 512, :])
                P.end_phase()
            with phase("b%d" % l):
                idx_sb = P.sb("idx", [128, 32], U32)
                Bidx = Buf()
                k_.dma("sp", idx_sb[:], E["idx"][:, :], P.dsem("idx"), (), [Bidx])
                cd = layer_d(l)
                if l + 1 < L_:
                    cd["w_in_n"] = E["w_in%d" % (l + 1)]
                cb, cs = tile_units(cd, True, l + 1 < L_)
                side = convert_gen(P, cb, cs, WBS[l], WSS[l])
                emit_attn(nc, P, None, None, None, None, gather=(G[l], idx_sb, Bidx), fused_out=(SA[l], SH[l]),
                          side=side, side_every=max(1, 1000 // (len(cb) + len(cs) + 4)))
                P.end_phase()
            with phase("xa%d" % l):
                for dest in range(4):
                    collective(SA[l][dest], A[l][dest * 512:(dest + 1) * 512, :])
                collective(SH[l].rearrange("s p c -> (s p) c"), AH[l][:, :])
                P.barrier()
                idx_sb = P.sb("idx", [128, 32], U32)
                Bidx = Buf()
                k_.dma("sp", idx_sb[:], E["idx"][:, :], P.dsem("idx"), (), [Bidx])
                bnc = Rot(P, "bnc", 2, [128, 4096], BF16)
                bnh = Rot(P, "bnh", 2, [128, 128], BF16)
                for h in range(4):
                    t, Bt = bnc.next()
                    P.op("pool", lambda e, t=t, h=h: e.indirect_dma_start(
                        out=t[:], out_offset=None, in_=A[l][:, :],
                        in_offset=IndirectOffsetOnAxis(ap=idx_sb[:, 12 + h:13 + h], axis=0)),
                        [Bidx], [Bt], dma=bnc.sem)
                    k_.dma("sp", ATL[l][h, :, HALO:NT], t[:], P.dsem("atl"), [Bt], ())
                    t2, Bt2 = bnh.next()
                    P.op("pool", lambda e, t2=t2, h=h: e.indirect_dma_start(
                        out=t2[:], out_offset=None, in_=AH[l][:, :],
                        in_offset=IndirectOffsetOnAxis(ap=idx_sb[:, 16 + h:17 + h], axis=0)),
                        [Bidx], [Bt2], dma=bnh.sem)
                    k_.dma("sp", ATL[l][h, :, 0:HALO], t2[:], P.dsem("atl"), [Bt2], ())
                if l > 0:
                    collective(X1[TOKC:NT, :], XH[:, :])
                    P.barrier()
                    xb = P.sb("xbnc", [128, D], F32)
                    Bxb = Buf()
                    P.op("pool", lambda e: e.indirect_dma_start(
                        out=xb[:], out_offset=None, in_=XH[:, :],
                        in_offset=IndirectOffsetOnAxis(ap=idx_sb[:, 20:21], axis=0)),
                        [Bidx], [Bxb], dma=P.dsem("xbnc"))
                    k_.dma("sp", X1[0:HALO, :], xb[:], P.dsem("xh"), [Bxb], ())
                P.end_phase()
            with phase("p%d" % (l + 1)):
                dd = layer_d(l)
                dd["xin"] = E["xin"] if l == 0 else X1
                dd["Wb"] = WBS[l]
                dd["Wsb"] = WSS[l]
                if not last:
                    dd.update(qkv_d(l + 1))
                    dd["xout"] = X1
                else:
                    dd["gfin"] = E["gfin"]
                    dd["out"] = out
                emit_P(nc, P, dd, True, not last, last)
                if last:
                    P.finish()
                P.end_phase()
    return nc


def core_idx_table(s):
    p = np.arange(128, dtype=np.int64)
    t = np.zeros((128, 32), np.int64)
    h = s
    for which in range(3):
        for src in range(4):
            t[:, which * 4 + src] = ((h * 3 + which) * 4 + src) * 128 + p
    for hh in range(4):
        t[:, 12 + hh] = (s * 4 + hh) * 128 + p
        t[:, 16 + hh] = (hh * 4 + s) * 128 + p
    t[:, 20] = max(s - 1, 0) * 128 + p
    return t.astype(np.uint32)


def kernel_fused(**inputs):
    W = {k: np.asarray(v) for k, v in inputs.items()}
    x = W["x"]
    nc = build_fused()
    bapp = np.ascontiguousarray(np.stack([pp_of(W["b_ada"][l], 96) for l in range(L_)], axis=1))
    bapp_s = [np.ascontiguousarray(bapp[:, :, s * 24:(s + 1) * 24]) for s in range(4)]
    wada_s = [np.ascontiguousarray(W["w_ada"][:, :, s * 3072:(s + 1) * 3072]) for s in range(4)]
    gfin = np.ascontiguousarray(np.broadcast_to(W["g_final"][None, :], (128, D)))
    lay = []
    dummy = np.zeros((128, 96), np.float32)
    for l in range(L_):
        m = layer_inputs(W, l, dummy)
        lay.append(m)
    in_maps = []
    for b in range(NB_):
        for s in range(4):
            m = {"xin": with_halo(x[b], s * TOKC, TOKC, axis=0),
                 "hmask": (np.zeros if s == 0 else np.ones)((128, 1), np.float32),
                 "cpp": pp_of(W["c"][b], 16), "bapp": bapp_s[s], "idx": core_idx_table(s), "w_ada": wada_s[s],
                 "gfin": gfin}
            for l in range(L_):
                m["pp%d" % l] = lay[l]["pp"]
                m["rb%d" % l] = lay[l]["rb"]
                m["wsT%d" % l] = lay[l]["wsT"]
                for nm, _ in LAYER_W:
                    m["%s%d" % (nm, l)] = lay[l][nm]
            in_maps.append(m)
    res = run_bass_kernel_spmd(nc, in_maps, core_ids=list(range(NCORE)))
    out = np.empty_like(x)
    for i in range(NCORE):
        b, s = divmod(i, 4)
        out[b, s * TOKC:(s + 1) * TOKC] = np.asarray(res.results[i]["out"])
    return out


def kernel(**inputs):
    return kernel_fused(**inputs)
```
